# Optimizing a Trainium2 kernel written in Bass

```python
import jax, jax.numpy as jnp
from jax import lax
import numpy as np

D_MODEL = 1024
BATCH = 8
SEQ = 2048
DEPTH = 4

N_MIXERS = 2
CHUNK = 128
GMLP_WIDTH = 2 * D_MODEL
GMLP_GROUPS = 8
GMLP_GROUP_DIM = GMLP_WIDTH // GMLP_GROUPS
FOX_HEAD_DIM = 64
FOX_HEADS = D_MODEL // FOX_HEAD_DIM
FOX_WIDTH = FOX_HEADS * FOX_HEAD_DIM
Q_BLOCK = 128
D_FF = 4 * D_MODEL
N_GMLP = (DEPTH + 1) // 2
N_FOX = DEPTH // 2
RMS_EPS = 1e-6
LN_EPS = 1e-5

kernel_name = "hybrid_gmlp_fox_sqrelu_trunk"


def rms_norm(x, g):
    xf = x.astype(jnp.float32)
    y = xf * lax.rsqrt(jnp.mean(xf * xf, axis=-1, keepdims=True) + RMS_EPS)
    return (y * g.astype(jnp.float32)).astype(x.dtype)


def layer_norm(x, g, b):
    xf = x.astype(jnp.float32)
    mu = jnp.mean(xf, axis=-1, keepdims=True)
    xc = xf - mu
    var = jnp.mean(xc * xc, axis=-1, keepdims=True)
    y = xc * lax.rsqrt(var + LN_EPS) * g.astype(jnp.float32) + b.astype(jnp.float32)
    return y.astype(x.dtype)


def gmlp_mixer(h, w_in, ln_g, ln_b, w_s, b_s, w_out):
    B, S, _ = h.shape
    z = jax.nn.gelu(h @ w_in)
    u, v = jnp.split(z, 2, axis=-1)
    v = layer_norm(v, ln_g, ln_b)
    v = v.reshape(B, S // CHUNK, CHUNK, GMLP_GROUPS, GMLP_GROUP_DIM)
    causal = jnp.tril(jnp.ones((CHUNK, CHUNK), dtype=bool))
    w = jnp.where(causal[None], w_s, jnp.zeros((), w_s.dtype))
    s = jnp.einsum('gts,bnsgc->bntgc', w, v) + b_s.T[None, None, :, :, None]
    out = u * s.reshape(B, S, GMLP_WIDTH)
    return out @ w_out


def fox_mixer(h, w_in, b_f, q_g, k_g, w_out):
    B, S, _ = h.shape
    W, H, Dh = FOX_WIDTH, FOX_HEADS, FOX_HEAD_DIM
    p = h @ w_in
    q, k, v, gate, f_logit = jnp.split(p, [W, 2 * W, 3 * W, 4 * W], axis=-1)
    q = rms_norm(q.reshape(B, S, H, Dh), q_g).transpose(0, 2, 1, 3)
    k = rms_norm(k.reshape(B, S, H, Dh), k_g).transpose(0, 2, 1, 3)
    v = v.reshape(B, S, H, Dh).transpose(0, 2, 1, 3)
    log_f = jax.nn.log_sigmoid((f_logit + b_f).astype(jnp.float32))
    c = jnp.cumsum(log_f, axis=1).transpose(0, 2, 1)
    scale = Dh ** -0.5
    outs = []
    for i in range(S // Q_BLOCK):
        q0 = i * Q_BLOCK
        kend = q0 + Q_BLOCK
        qb = q[:, :, q0:kend]
        kb = k[:, :, :kend]
        vb = v[:, :, :kend]
        logits = jnp.einsum('bhtd,bhsd->bhts', qb, kb).astype(jnp.float32) * scale
        logits = logits + c[:, :, q0:kend, None] - c[:, :, None, :kend]
        t_idx = q0 + jnp.arange(Q_BLOCK)[:, None]
        s_idx = jnp.arange(kend)[None, :]
        logits = jnp.where(t_idx >= s_idx, logits, -jnp.inf)
        probs = jax.nn.softmax(logits, axis=-1).astype(vb.dtype)
        outs.append(jnp.einsum('bhts,bhsd->bhtd', probs, vb))
    o = jnp.concatenate(outs, axis=2).transpose(0, 2, 1, 3).reshape(B, S, W)
    o = o * jax.nn.sigmoid(gate)
    return o @ w_out


def sqrelu_mlp(h, w1, w2):
    return jnp.square(jax.nn.relu(h @ w1)) @ w2


def setup_inputs(seed: int = 0) -> dict:
    key = jax.random.key(seed)
    ks = jax.random.split(key, 20)
    f32 = jnp.float32
    D, E, G, W, H, Dh = D_MODEL, GMLP_WIDTH, GMLP_GROUPS, FOX_WIDTH, FOX_HEADS, FOX_HEAD_DIM
    nrm = lambda k, shape, s: jax.random.normal(k, shape, f32) * s
    x = nrm(ks[0], (BATCH, SEQ, D), 1.0)
    gmlp_w_in = nrm(ks[1], (N_GMLP, D, 2 * E), D ** -0.5)
    gmlp_ln_g = 1.0 + nrm(ks[2], (N_GMLP, E), 0.02)
    gmlp_ln_b = nrm(ks[3], (N_GMLP, E), 0.02)
    gmlp_w_s = nrm(ks[4], (N_GMLP, G, CHUNK, CHUNK), CHUNK ** -0.5)
    gmlp_b_s = 1.0 + nrm(ks[5], (N_GMLP, G, CHUNK), 0.1)
    gmlp_w_out = nrm(ks[6], (N_GMLP, E, D), E ** -0.5)
    fox_w_in = nrm(ks[7], (N_FOX, D, 4 * W + H), D ** -0.5)
    fox_b_f = jnp.linspace(0.0, 4.0, H, dtype=f32)[None, :] + nrm(ks[8], (N_FOX, H), 0.1)
    fox_q_g = 1.0 + nrm(ks[9], (N_FOX, Dh), 0.02)
    fox_k_g = 1.0 + nrm(ks[10], (N_FOX, Dh), 0.02)
    fox_w_out = nrm(ks[11], (N_FOX, W, D), W ** -0.5)
    mix_norm_g = 1.0 + nrm(ks[12], (DEPTH, D), 0.02)
    mlp_norm_g = 1.0 + nrm(ks[13], (DEPTH, D), 0.02)
    mlp_w1 = nrm(ks[14], (DEPTH, D, D_FF), D ** -0.5)
    mlp_w2 = nrm(ks[15], (DEPTH, D_FF, D), D_FF ** -0.5)
    return {"x": x, "gmlp_w_in": gmlp_w_in, "gmlp_ln_g": gmlp_ln_g, "gmlp_ln_b": gmlp_ln_b,
            "gmlp_w_s": gmlp_w_s, "gmlp_b_s": gmlp_b_s, "gmlp_w_out": gmlp_w_out,
            "fox_w_in": fox_w_in, "fox_b_f": fox_b_f, "fox_q_g": fox_q_g, "fox_k_g": fox_k_g,
            "fox_w_out": fox_w_out, "mix_norm_g": mix_norm_g, "mlp_norm_g": mlp_norm_g,
            "mlp_w1": mlp_w1, "mlp_w2": mlp_w2}


def reference(x, gmlp_w_in, gmlp_ln_g, gmlp_ln_b, gmlp_w_s, gmlp_b_s, gmlp_w_out,
              fox_w_in, fox_b_f, fox_q_g, fox_k_g, fox_w_out,
              mix_norm_g, mlp_norm_g, mlp_w1, mlp_w2):
    for i in range(DEPTH):
        h = rms_norm(x, mix_norm_g[i])
        j = i // N_MIXERS
        if i % N_MIXERS == 0:
            x = x + gmlp_mixer(h, gmlp_w_in[j], gmlp_ln_g[j], gmlp_ln_b[j],
                               gmlp_w_s[j], gmlp_b_s[j], gmlp_w_out[j])
        else:
            x = x + fox_mixer(h, fox_w_in[j], fox_b_f[j], fox_q_g[j], fox_k_g[j], fox_w_out[j])
        h = rms_norm(x, mlp_norm_g[i])
        x = x + sqrelu_mlp(h, mlp_w1[i], mlp_w2[i])
    return x
```

```python
import numpy as np
from contextlib import ExitStack

import concourse.bass as bass
import concourse.mybir as mybir
from concourse.bass_utils import run_bass_kernel_spmd

F32 = mybir.dt.float32
BF16 = mybir.dt.bfloat16
AF = mybir.ActivationFunctionType
ALU = mybir.AluOpType

S = 2048
D = 1024
E = 2048
DFF = 4096
H = 16
DH = 64
DEPTH = 4
NCORES = 8
RMS_EPS = 1e-6
LN_EPS = 1e-5
MASK_NEG = -30000.0

FULL = [("gmlp", 0), ("mlp", 0), ("fox", 1), ("mlp", 1), ("gmlp", 2), ("mlp", 2), ("fox", 3), ("mlp", 3)]
LAUNCHES = [FULL[0:2], FULL[2:4], FULL[4:6], FULL[6:8]]


def _dsize(dt):
    return 4 if dt == F32 else 2


class _Rec:
    __slots__ = ("box", "w", "r")

    def __init__(self, box, w, r):
        self.box = box
        self.w = w
        self.r = r


class Prog:
    def __init__(self, nc, es):
        self.nc = nc
        self.es = es
        self.eng = {"pe": nc.tensor, "act": nc.scalar, "dve": nc.vector, "pool": nc.gpsimd, "sp": nc.sync}
        self.sem = {}
        self.cnt = {}
        for n in ("pe", "act", "dve", "pool"):
            self.sem[n] = es.enter_context(nc.semaphore("s_" + n))
            self.cnt[n] = 0
        self.waited = {n: {} for n in self.eng}
        self.recs = {}
        self.nwaits = 0
        self.ninst = 0

    @staticmethod
    def box(ap):
        t = ap.tensor
        tn = type(t).__name__
        if tn.startswith("DRam"):
            return None
        shp = list(t.shape)
        pitch = _dsize(t.dtype)
        for s_ in shp[1:]:
            pitch *= int(s_)
        es_ = _dsize(ap.dtype)
        offb = int(ap.offset) * es_
        p0 = offb // pitch
        lo = offb % pitch
        pat = ap.ap
        pstep, pcnt = pat[0]
        p1 = p0 + (int(pcnt) if pstep != 0 else 1)
        ext = 1
        for st, c in pat[1:]:
            ext += (int(c) - 1) * abs(int(st))
        return (t.name, p0, p1, lo, lo + ext * es_)

    @staticmethod
    def _ov(a, b):
        return a[1] < b[2] and b[1] < a[2] and a[3] < b[4] and b[3] < a[4]

    @staticmethod
    def _contains(o, i):
        return o[1] <= i[1] and i[2] <= o[2] and o[3] <= i[3] and i[4] <= o[4]

    def _deps(self, stream, reads, writes):
        deps = {}

        def add(dep, kind):
            sk, v = dep
            if sk == stream and stream == "pe":
                return
            if v > deps.get(sk, 0):
                deps[sk] = v

        for ap in reads:
            b = self.box(ap)
            if b is None:
                continue
            for rec in self.recs.get(b[0], ()):
                if rec.w is not None and self._ov(rec.box, b):
                    add(rec.w, "RAW")
        for ap in writes:
            b = self.box(ap)
            if b is None:
                continue
            for rec in self.recs.get(b[0], ()):
                if self._ov(rec.box, b):
                    if rec.w is not None:
                        add(rec.w, "WAW")
                    for sk, v in rec.r.items():
                        add((sk, v), "WAR")
        return deps

    def _commit(self, reads, writes, dep):
        sk, v = dep
        for ap in writes:
            b = self.box(ap)
            if b is None:
                continue
            lst = self.recs.setdefault(b[0], [])
            lst[:] = [r for r in lst if not self._contains(b, r.box)]
            lst.append(_Rec(b, dep, {}))
        for ap in reads:
            b = self.box(ap)
            if b is None:
                continue
            lst = self.recs.setdefault(b[0], [])
            for r in lst:
                if r.box == b:
                    if v > r.r.get(sk, 0):
                        r.r[sk] = v
                    break
            else:
                lst.append(_Rec(b, None, {sk: v}))

    def _wait(self, stream, sk, v):
        w = self.waited[stream]
        if w.get(sk, 0) >= v:
            return
        w[sk] = v
        self.eng[stream].wait_ge(self.sem[sk], v)
        self.nwaits += 1

    def op(self, ename, fn, reads=(), writes=(), inc=True):
        deps = self._deps(ename, reads, writes)
        for sk, v in deps.items():
            self._wait(ename, sk, v)
        ins = fn(self.eng[ename])
        self.ninst += 1
        if inc:
            self.cnt[ename] += 1
            ins.then_inc(self.sem[ename], 1)
            mark = self.cnt[ename]
        else:
            mark = self.cnt[ename] + 1
        self._commit(reads, writes, (ename, mark))
        return ins

    def dsem(self, key):
        if key not in self.sem:
            self.sem[key] = self.es.enter_context(self.nc.semaphore("d_" + key))
            self.cnt[key] = 0
        return self.sem[key]

    def dma(self, q, pairs, key, extra_deps=()):
        sem = self.dsem(key)
        reads = [i for _, i in pairs]
        writes = [o for o, _ in pairs]
        deps = self._deps(q, reads, writes)
        for sk, v in extra_deps:
            if v > deps.get(sk, 0):
                deps[sk] = v
        if self.cnt[key] > 0:
            deps[key] = max(deps.get(key, 0), self.cnt[key])
        for sk, v in deps.items():
            self._wait(q, sk, v)
        for o, i in pairs:
            self.eng[q].dma_start(out=o, in_=i).then_inc(sem, 16)
            self.cnt[key] += 16
            self.ninst += 1
        dep = (key, self.cnt[key])
        self._commit(reads, writes, dep)
        return dep

    def wait_all(self, stream, keys):
        for k in keys:
            if self.cnt.get(k, 0) > 0:
                self._wait(stream, k, self.cnt[k])


class Ctx:
    pass


def _declare_inputs(nc, sublayers):
    d = {}

    def inp(name, shape):
        d[name] = nc.dram_tensor(name, list(shape), F32, kind="ExternalInput").ap()

    inp("xT", [D, S])
    inp("g_mix", [128, DEPTH, 8])
    inp("g_mlp", [128, DEPTH, 8])
    kinds = set(k for k, _ in sublayers)
    layers = sorted(set(l for _, l in sublayers))
    for k, l in sublayers:
        if k == "mlp":
            inp(f"w1_{l}", [D, DFF])
            inp(f"w2r_{l}", [8, 128, 32 * 128])
        elif k == "gmlp":
            inp(f"gwin_{l}", [D, 2 * E])
            inp(f"gwor_{l}", [4, 128, 16 * 256])
            inp(f"glng_{l}", [1, E])
            inp(f"glnb_{l}", [1, E])
            inp(f"gwsT_{l}", [128, 8, 128])
            inp(f"gbs_{l}", [1, 8 * 128])
        elif k == "fox":
            inp(f"fqkg_{l}", [8, 128, 8 * 384])
            inp(f"fwv_{l}", [D, D])
            inp(f"fwf_{l}", [128, 8 * 16])
            inp(f"fwo_{l}", [D, D])
            inp(f"fbf_{l}", [16, 1])
            inp(f"fgq_{l}", [128, 1])
            inp(f"fgk_{l}", [128, 1])
    return d


def prep_inputs(inputs, b, sublayers, x_override=None):
    f = lambda a: np.ascontiguousarray(a, dtype=np.float32)
    m = {}
    xb = inputs["x"][b] if x_override is None else x_override
    m["xT"] = f(np.asarray(xb).T)
    m["g_mix"] = f(np.asarray(inputs["mix_norm_g"]).reshape(DEPTH, 8, 128).transpose(2, 0, 1))
    m["g_mlp"] = f(np.asarray(inputs["mlp_norm_g"]).reshape(DEPTH, 8, 128).transpose(2, 0, 1))
    for k, l in sublayers:
        j = l // 2
        if k == "mlp":
            m[f"w1_{l}"] = f(inputs["mlp_w1"][l])
            w2 = np.asarray(inputs["mlp_w2"][l]).reshape(32, 128, 8, 128)
            m[f"w2r_{l}"] = f(w2.transpose(2, 1, 0, 3).reshape(8, 128, 32 * 128))
        elif k == "gmlp":
            m[f"gwin_{l}"] = f(inputs["gmlp_w_in"][j])
            wo = np.asarray(inputs["gmlp_w_out"][j]).reshape(16, 128, 4, 256)
            m[f"gwor_{l}"] = f(wo.transpose(2, 1, 0, 3).reshape(4, 128, 16 * 256))
            m[f"glng_{l}"] = f(np.asarray(inputs["gmlp_ln_g"][j]).reshape(1, E))
            m[f"glnb_{l}"] = f(np.asarray(inputs["gmlp_ln_b"][j]).reshape(1, E))
            ws = np.asarray(inputs["gmlp_w_s"][j])
            m[f"gwsT_{l}"] = f(ws.transpose(2, 0, 1))
            m[f"gbs_{l}"] = f(np.asarray(inputs["gmlp_b_s"][j]).reshape(1, 8 * 128))
        elif k == "fox":
            wi = np.asarray(inputs["fox_w_in"][j])
            q = wi[:, 0:1024].reshape(8, 128, 8, 128)
            kk = wi[:, 1024:2048].reshape(8, 128, 8, 128)
            gt = wi[:, 3072:4096].reshape(8, 128, 8, 128)
            qkg = np.stack([q, kk, gt], axis=3)
            m[f"fqkg_{l}"] = f(qkg.transpose(2, 1, 0, 3, 4).reshape(8, 128, 8 * 384))
            m[f"fwv_{l}"] = f(wi[:, 2048:3072])
            wf = wi[:, 4096:4112].reshape(8, 128, 16)
            m[f"fwf_{l}"] = f(wf.transpose(1, 0, 2).reshape(128, 8 * 16))
            m[f"fwo_{l}"] = f(inputs["fox_w_out"][j])
            m[f"fbf_{l}"] = f(np.asarray(inputs["fox_b_f"][j]).reshape(16, 1))
            m[f"fgq_{l}"] = f(np.tile(np.asarray(inputs["fox_q_g"][j]), 2).reshape(128, 1))
            m[f"fgk_{l}"] = f(np.tile(np.asarray(inputs["fox_k_g"][j]), 2).reshape(128, 1))
    return m


def build_program(sublayers):
    nc = bass.Bass("TRN2", target_bir_lowering=False)
    es = ExitStack()
    din = _declare_inputs(nc, sublayers)
    outT = nc.dram_tensor("outT", [D, S], F32, kind="ExternalOutput").ap()
    p = Prog(nc, es)
    c = Ctx()
    c.nc, c.p, c.din = nc, p, din

    sb = lambda name, shape, dt: es.enter_context(nc.sbuf_tensor(name, list(shape), dt))
    XT = sb("XT", [128, 8, S], F32)
    HT = sb("HT", [128, 8, S], BF16)
    RING = sb("RING", [128, 4, 4096], BF16)
    CB = sb("CB", [128, 6, 128], BF16)
    CF = sb("CF", [128, 96], F32)
    ARENA_BYTES = 78000
    AR = sb("AR", [128, ARENA_BYTES // 2], BF16)
    PS = es.enter_context(nc.psum_tensor("PS", [128, 8, 512], F32))
    c.XT, c.HT, c.RING, c.AR, c.PS = XT, HT, RING, AR, PS
    c.ONES, c.IDENT, c.ZERO, c.MASK, c.BD, c.SEL = (CB[:, i, :] for i in range(6))
    c.GMIX = CF[:, 0:32].rearrange("p (l k) -> p l k", l=DEPTH)
    c.GMLP = CF[:, 32:64].rearrange("p (l k) -> p l k", l=DEPTH)
    c.EPS_RMS = CF[:, 64:65]
    c.EPS_LN = CF[:, 65:66]
    c.GQ = CF[:, 66:67]
    c.GK = CF[:, 67:68]
    c.NBF = CF[:, 68:69]
    c.CF = CF
    c.bank_i = 0
    c.ring_i = 0

    def arena_bf(off, n):
        assert off % 2 == 0 and off + 2 * n <= ARENA_BYTES, (off, n)
        return AR[:, off // 2: off // 2 + n]

    def arena_f32(off, n):
        assert off % 4 == 0 and off + 4 * n <= ARENA_BYTES, (off, n)
        return AR[:, off // 2: off // 2 + 2 * n].bitcast(F32)

    c.abf, c.af32 = arena_bf, arena_f32

    def ring_f32(slot, off, n):
        return RING[:, slot, off // 2: off // 2 + 2 * n].bitcast(F32)

    c.ring_f32 = ring_f32

    V = "pool"
    p.op(V, lambda e: e.memset(c.ONES, 1.0), writes=[c.ONES])
    p.op(V, lambda e: e.memset(c.ZERO, 0.0), writes=[c.ZERO])
    p.op(V, lambda e: e.affine_select(out=c.IDENT, in_=c.ONES, pattern=[[1, 128]], compare_op=ALU.is_equal,
                                      fill=0.0, base=0, channel_multiplier=-1), reads=[c.ONES], writes=[c.IDENT])
    p.op(V, lambda e: e.affine_select(out=c.MASK, in_=c.ZERO, pattern=[[1, 128]], compare_op=ALU.is_ge,
                                      fill=MASK_NEG, base=0, channel_multiplier=-1), reads=[c.ZERO], writes=[c.MASK])
    p.op(V, lambda e: e.memset(c.BD, 0.0), writes=[c.BD])
    p.op(V, lambda e: e.memset(CB[0:64, 4, 0:64], 1.0), writes=[CB[0:64, 4, 0:64]])
    p.op(V, lambda e: e.memset(CB[64:128, 4, 64:128], 1.0), writes=[CB[64:128, 4, 64:128]])
    p.op(V, lambda e: e.memset(c.SEL, 0.0), writes=[c.SEL])
    p.op(V, lambda e: e.memset(CB[0:1, 5, :], 1.0), writes=[CB[0:1, 5, :]])
    p.op(V, lambda e: e.memset(CB[32:33, 5, :], 1.0), writes=[CB[32:33, 5, :]])
    p.op(V, lambda e: e.memset(c.EPS_RMS, RMS_EPS), writes=[c.EPS_RMS])
    p.op(V, lambda e: e.memset(c.EPS_LN, LN_EPS), writes=[c.EPS_LN])

    xv = din["xT"].rearrange("(k p) t -> p k t", p=128)
    p.dma("sp", [(XT[:, k, :], xv[:, k, :]) for k in range(8)], "xin")
    p.dma("sp", [(CF[:, 0:32], din["g_mix"].rearrange("p l k -> p (l k)")),
                 (CF[:, 32:64], din["g_mlp"].rearrange("p l k -> p (l k)"))], "misc")

    for kind, l in sublayers:
        if kind == "mlp":
            rmsnorm(c, c.GMLP[:, l, :])
            mlp_phase(c, l)
        elif kind == "gmlp":
            rmsnorm(c, c.GMIX[:, l, :])
            gmlp_phase(c, l)
        elif kind == "fox":
            rmsnorm(c, c.GMIX[:, l, :])
            fox_phase(c, l)

    ov = outT.rearrange("(k p) t -> p k t", p=128)
    p.dma("sp", [(ov[:, k, :], XT[:, k, :]) for k in range(8)], "xout")
    p.wait_all("sp", ["xout"])
    for st in ("pe", "act", "dve", "pool", "sp"):
        p.wait_all(st, [k for k in ("pe", "act", "dve", "pool") if k != st])
    c.es = es
    return nc, c


def next_bank(c, pool=None):
    pool = pool or (0, 1, 2, 3, 4, 5, 6, 7)
    b = pool[c.bank_i % len(pool)]
    c.bank_i += 1
    return c.PS[:, b, :]


def next_slot(c, nslots=4):
    s = c.ring_i % nslots
    c.ring_i += 1
    return s


def load_w(c, src, shape):
    s = next_slot(c, c.ring_slots)
    n = shape[1] * shape[2]
    assert n <= 4096
    dst = c.RING[:, s, 0:n].rearrange("p (a b) -> p a b", a=shape[1])
    c.p.dma("pool", [(dst, src)], f"ring{s}")
    return dst


def rmsnorm(c, G):
    p = c.p
    SQ = c.abf(0, 8 * 512).rearrange("p (k t) -> p k t", k=8)
    LNT = c.af32(8192, 512)
    for tb in range(4):
        ts_ = slice(tb * 512, (tb + 1) * 512)
        xin = c.XT[:, :, ts_]
        p.op("act", lambda e: e.activation(out=SQ, in_=xin, func=AF.Square), reads=[xin], writes=[SQ])
        bank = next_bank(c)
        for k in range(8):
            p.op("pe", lambda e: e.matmul(bank, lhsT=c.ONES, rhs=SQ[:, k, :], start=(k == 0), stop=(k == 7)),
                 reads=[c.ONES, SQ[:, k, :]], writes=[bank], inc=(k == 7))
        p.op("act", lambda e: e.activation(out=LNT, in_=bank, func=AF.Ln, scale=1.0 / D, bias=c.EPS_RMS),
             reads=[bank, c.EPS_RMS], writes=[LNT])
        p.op("act", lambda e: e.activation(out=LNT, in_=LNT, func=AF.Exp, scale=-0.5), reads=[LNT], writes=[LNT])
        for k in range(8):
            o = c.HT[:, k, ts_]
            i0 = c.XT[:, k, ts_]
            p.op("dve", lambda e: e.scalar_tensor_tensor(out=o, in0=i0, scalar=G[:, k:k + 1], in1=LNT,
                                                         op0=ALU.mult, op1=ALU.mult),
                 reads=[i0, G[:, k:k + 1], LNT], writes=[o])


def mlp_phase(c, l):
    p = c.p
    c.ring_slots = 4
    HID = c.abf(0, 32 * 1024).rearrange("p (f t) -> p f t", f=32)
    TMP = [c.af32(65536 + 2048 * i, 512) for i in range(2)]
    w1 = c.din[f"w1_{l}"].rearrange("(k p) f -> p k f", p=128)
    w2r = c.din[f"w2r_{l}"]
    ti = 0
    for tbk in range(2):
        t0 = tbk * 1024
        for fg in range(8):
            W = load_w(c, w1[:, :, fg * 512:(fg + 1) * 512], [128, 8, 512])
            for fi in range(4):
                fc = fg * 4 + fi
                for th in range(2):
                    bank = next_bank(c)
                    rsl = slice(t0 + th * 512, t0 + (th + 1) * 512)
                    for k in range(8):
                        lw = W[:, k, fi * 128:(fi + 1) * 128]
                        rh = c.HT[:, k, rsl]
                        p.op("pe", lambda e: e.matmul(bank, lhsT=lw, rhs=rh, start=(k == 0), stop=(k == 7)),
                             reads=[lw, rh], writes=[bank], inc=(k == 7))
                    tmp = TMP[ti % 2]
                    ti += 1
                    p.op("act", lambda e: e.activation(out=tmp, in_=bank, func=AF.Relu), reads=[bank], writes=[tmp])
                    ho = HID[:, fc, th * 512:(th + 1) * 512]
                    p.op("dve", lambda e: e.tensor_tensor(out=ho, in0=tmp, in1=tmp, op=ALU.mult),
                         reads=[tmp], writes=[ho])
        for dc in range(8):
            W2 = load_w(c, w2r[dc], [128, 32, 128])
            for th in range(2):
                bank = next_bank(c)
                for fc in range(32):
                    lw = W2[:, fc, :]
                    rh = HID[:, fc, th * 512:(th + 1) * 512]
                    p.op("pe", lambda e: e.matmul(bank, lhsT=lw, rhs=rh, start=(fc == 0), stop=(fc == 31)),
                         reads=[lw, rh], writes=[bank], inc=(fc == 31))
                xo = c.XT[:, dc, t0 + th * 512:t0 + (th + 1) * 512]
                p.op("dve", lambda e: e.tensor_tensor(out=xo, in0=bank, in1=xo, op=ALU.add),
                     reads=[bank, xo], writes=[xo])


def gmlp_phase(c, l):
    p = c.p
    c.ring_slots = 4
    OFF = 0
    UT = c.abf(0, 16 * 1024).rearrange("p (c t) -> p c t", c=16)
    VG = [c.af32(32768 + 8192 * i, 2048) for i in range(2)]
    VLN = c.abf(49152, 2048)
    LNG = c.af32(53248, 2048)
    LNB = c.af32(61440, 2048)
    WMT = c.abf(69632, 1024).rearrange("p (g t) -> p g t", g=8)
    BS2 = c.abf(71680, 1024).rearrange("p (g t) -> p g t", g=8)
    WSF = c.af32(73728, 1024).rearrange("p (g t) -> p g t", g=8)
    ST = c.CF[:, 69:69 + 24]
    MV = c.CF[:, 93:95]
    RSTD = c.CF[:, 95:96]
    NMR = c.CF[:, 94:95]
    win = c.din[f"gwin_{l}"].rearrange("(k p) f -> p k f", p=128)
    wor = c.din[f"gwor_{l}"]

    p.dma("sp", [(LNG, c.din[f"glng_{l}"].partition_broadcast(128).rearrange("p o e -> p (o e)")),
                 (LNB, c.din[f"glnb_{l}"].partition_broadcast(128).rearrange("p o e -> p (o e)")),
                 (WSF, c.din[f"gwsT_{l}"])], "misc")
    p.op("pool", lambda e: e.affine_select(out=WMT, in_=WSF, pattern=[[0, 8], [1, 128]], compare_op=ALU.is_ge,
                                           fill=0.0, base=0, channel_multiplier=-1), reads=[WSF], writes=[WMT])
    BSF = c.af32(32768, 1024)
    BSH = c.abf(32768 + 4096, 1024)
    p.op("pool", lambda e: e.memset(BS2[0:64], 0.0), writes=[BS2[0:64]])
    p.dma("sp", [(BSF[0:1, :], c.din[f"gbs_{l}"]), (BSF[32:33, :], c.din[f"gbs_{l}"])], "misc")
    bs2f = BS2.rearrange("p g t -> p (g t)")
    p.op("dve", lambda e: e.tensor_copy(out=bs2f[0:1, :], in_=BSF[0:1, :]), reads=[BSF[0:1, :]], writes=[bs2f[0:1, :]])
    p.op("dve", lambda e: e.tensor_copy(out=BSH[32:33, :], in_=BSF[32:33, :]), reads=[BSF[32:33, :]],
         writes=[BSH[32:33, :]])
    p.op("dve", lambda e: e.tensor_tensor(out=bs2f[32:33, :], in0=BSF[32:33, :], in1=BSH[32:33, :], op=ALU.subtract),
         reads=[BSF[32:33, :], BSH[32:33, :]], writes=[bs2f[32:33, :]])

    import os as _os
    _stop = _os.environ.get("GMLP_STOP", "")
    if _stop == "setup":
        return
    vi = 0
    for tbk in range(2):
        t0 = tbk * 1024
        for ug in range(4):
            W = load_w(c, win[:, :, ug * 512:(ug + 1) * 512], [128, 8, 512])
            for ci in range(4):
                cc = ug * 4 + ci
                for th in range(2):
                    bank = next_bank(c)
                    rsl = slice(t0 + th * 512, t0 + (th + 1) * 512)
                    for k in range(8):
                        lw = W[:, k, ci * 128:(ci + 1) * 128]
                        rh = c.HT[:, k, rsl]
                        p.op("pe", lambda e: e.matmul(bank, lhsT=lw, rhs=rh, start=(k == 0), stop=(k == 7)),
                             reads=[lw, rh], writes=[bank], inc=(k == 7))
                    uo = UT[:, cc, th * 512:(th + 1) * 512]
                    p.op("act", lambda e: e.activation(out=uo, in_=bank, func=AF.Gelu_apprx_tanh),
                         reads=[bank], writes=[uo])
        if _stop == "U" or (_stop == "U2" and tbk == 1):
            return
        WV = [load_w(c, win[:, :, E + cg * 512:E + (cg + 1) * 512], [128, 8, 512]) for cg in range(4)]
        for n in range(8):
            tt = slice(t0 + n * 128, t0 + (n + 1) * 128)
            vg = VG[vi % 2]
            vi += 1
            for cg in range(4):
                bank = next_bank(c)
                for k in range(8):
                    lw = c.HT[:, k, tt]
                    rh = WV[cg][:, k, :]
                    p.op("pe", lambda e: e.matmul(bank, lhsT=lw, rhs=rh, start=(k == 0), stop=(k == 7)),
                         reads=[lw, rh], writes=[bank], inc=(k == 7))
                vo = vg[:, cg * 512:(cg + 1) * 512]
                p.op("act", lambda e: e.activation(out=vo, in_=bank, func=AF.Gelu_apprx_tanh),
                     reads=[bank], writes=[vo])
                so = ST[:, cg * 6:(cg + 1) * 6]
                p.op("dve", lambda e: e.bn_stats(out=so, in_=vo), reads=[vo], writes=[so])
            p.op("dve", lambda e: e.bn_aggr(out=MV, in_=ST), reads=[ST], writes=[MV])
            p.op("act", lambda e: e.activation(out=RSTD, in_=MV[:, 1:2], func=AF.Ln, bias=c.EPS_LN, scale=1.0),
                 reads=[MV[:, 1:2], c.EPS_LN], writes=[RSTD])
            p.op("act", lambda e: e.activation(out=RSTD, in_=RSTD, func=AF.Exp, scale=-0.5), reads=[RSTD], writes=[RSTD])
            p.op("dve", lambda e: e.scalar_tensor_tensor(out=NMR, in0=MV[:, 0:1], scalar=-1.0, in1=RSTD,
                                                         op0=ALU.mult, op1=ALU.mult),
                 reads=[MV[:, 0:1], RSTD], writes=[NMR])
            p.op("act", lambda e: e.activation(out=vg, in_=vg, func=AF.Identity, scale=RSTD, bias=NMR),
                 reads=[vg, RSTD, NMR], writes=[vg])
            p.op("dve", lambda e: e.tensor_tensor(out=vg, in0=vg, in1=LNG, op=ALU.mult), reads=[vg, LNG], writes=[vg])
            p.op("pool", lambda e: e.tensor_tensor(out=VLN, in0=vg, in1=LNB, op=ALU.add), reads=[vg, LNB], writes=[VLN])
            if _stop == "V" or (_stop == "V2" and tbk == 1):
                continue
            for cq in range(4):
                bank = next_bank(c)
                for ci in range(4):
                    cc = cq * 4 + ci
                    g = cc // 2
                    bo = bank[:, ci * 128:(ci + 1) * 128]
                    lw = VLN[:, cc * 128:(cc + 1) * 128]
                    rh = WMT[:, g, :]
                    p.op("pe", lambda e: e.matmul(bo, lhsT=lw, rhs=rh, start=True, stop=False),
                         reads=[lw, rh], writes=[bo], inc=False)
                    lw2 = c.SEL[0:64, :]
                    rh2 = BS2[0:64, g, :]
                    p.op("pe", lambda e: e.matmul(bo, lhsT=lw2, rhs=rh2, start=False, stop=True),
                         reads=[lw2, rh2], writes=[bo], inc=(ci == 3))
                uo = UT[:, cq * 4:(cq + 1) * 4, n * 128:(n + 1) * 128]
                bi = bank.rearrange("p (a b) -> p a b", a=4)
                p.op("dve", lambda e: e.tensor_tensor(out=uo, in0=bi, in1=uo, op=ALU.mult), reads=[bank, uo], writes=[uo])
        if _stop in ("V", "S") or (_stop in ("V2", "S2") and tbk == 1):
            continue
        _o2 = _os.environ.get("GMLP_O2", "all") if tbk == 1 else "all"
        for dcp in range(4):
            WO = load_w(c, wor[dcp], [128, 16, 256])
            if _o2 == "loads":
                continue
            for di in range(2):
                dc = dcp * 2 + di
                for th in range(2):
                    bank = next_bank(c)
                    for kc in range(16):
                        lw = WO[:, kc, di * 128:(di + 1) * 128]
                        rh = UT[:, kc, th * 512:(th + 1) * 512]
                        p.op("pe", lambda e: e.matmul(bank, lhsT=lw, rhs=rh, start=(kc == 0), stop=(kc == 15)),
                             reads=[lw, rh], writes=[bank], inc=(kc == 15))
                    if _o2 == "mm":
                        continue
                    xo = c.XT[:, dc, t0 + th * 512:t0 + (th + 1) * 512]
                    p.op("dve", lambda e: e.tensor_tensor(out=xo, in0=bank, in1=xo, op=ALU.add),
                         reads=[bank, xo], writes=[xo])
        if _stop == "O1":
            return


def fox_phase(c, l):
    p = c.p
    nc = c.nc
    c.ring_slots = 3
    POOL_O = (0, 1)
    POOL_L = (2, 3, 4)
    POOL_P = (5, 6, 7)
    VA = c.abf(0, 8 * 16 * 192).rearrange("p (j s c) -> p j s c", j=8, s=16)
    QK = [c.abf(49152 + 4096 * i, 2048) for i in range(4)]
    QE, QO, KE, KO = QK
    SG = c.abf(65536, 2048)
    PT = [c.abf(69632 + 1024 * i, 512) for i in range(3)]
    OG0 = c.abf(72704, 2048)
    SQb = c.RING[:, 3, 0:512]
    LNT = c.ring_f32(3, 1024, 512)
    LND = c.ring_f32(3, 3072, 512)
    O1 = c.ring_f32(3, 5120, 512)
    LF = c.af32(0, 2048)
    CS = c.af32(8192, 2048)
    ONF = c.af32(16384, 2048)
    R1 = c.af32(24576, 2048)
    CH = c.abf(32768, 3 * 2048).rearrange("p (a t) -> p a t", a=3)
    chd = nc.dram_tensor(f"chd_{l}", [16, 3, 2048], BF16, kind="Internal").ap()

    def og(j):
        if j == 0:
            return OG0
        return VA[:, j - 1].rearrange("p s c -> p (s c)")[:, 0:2048]

    p.dma("sp", [(c.GQ, c.din[f"fgq_{l}"]), (c.GK, c.din[f"fgk_{l}"]), (c.NBF[0:16, :], c.din[f"fbf_{l}"])], "misc")
    p.op("dve", lambda e: e.tensor_scalar(out=c.GQ, in0=c.GQ, scalar1=DH ** -0.5, scalar2=None, op0=ALU.mult),
         reads=[c.GQ], writes=[c.GQ])
    p.op("dve", lambda e: e.tensor_scalar(out=c.NBF[0:16, :], in0=c.NBF[0:16, :], scalar1=-1.0, scalar2=None,
                                          op0=ALU.mult), reads=[c.NBF[0:16, :]], writes=[c.NBF[0:16, :]])

    WF = load_w(c, c.din[f"fwf_{l}"].rearrange("p (k h) -> p k h", k=8), [128, 8, 16])
    p.op("pool", lambda e: e.memset(ONF[0:16, :], 1.0), writes=[ONF[0:16, :]])
    for tb in range(4):
        ts_ = slice(tb * 512, (tb + 1) * 512)
        bank = next_bank(c, POOL_P)
        for k in range(8):
            lw = WF[:, k, :]
            rh = c.HT[:, k, ts_]
            p.op("pe", lambda e: e.matmul(bank[0:16, :], lhsT=lw, rhs=rh, start=(k == 0), stop=(k == 7)),
                 reads=[lw, rh], writes=[bank[0:16, :]], inc=(k == 7))
        lo = LF[0:16, ts_]
        p.op("act", lambda e: e.activation(out=lo, in_=bank[0:16, :], func=AF.Exp, scale=-1.0, bias=c.NBF[0:16, :]),
             reads=[bank[0:16, :], c.NBF[0:16, :]], writes=[lo])
        p.op("act", lambda e: e.activation(out=lo, in_=lo, func=AF.Ln, scale=1.0, bias=1.0), reads=[lo], writes=[lo])
    p.op("dve", lambda e: e.tensor_tensor_scan(out=CS[0:16, :], data0=ONF[0:16, :], data1=LF[0:16, :], initial=0.0,
                                               op0=ALU.mult, op1=ALU.add),
         reads=[ONF[0:16, :], LF[0:16, :]], writes=[CS[0:16, :]])
    p.op("dve", lambda e: e.tensor_copy(out=CH[0:16, 0, :], in_=CS[0:16, :]), reads=[CS[0:16, :]], writes=[CH[0:16, 0, :]])
    p.op("dve", lambda e: e.tensor_tensor(out=R1[0:16, :], in0=CS[0:16, :], in1=CH[0:16, 0, :], op=ALU.subtract),
         reads=[CS[0:16, :], CH[0:16, 0, :]], writes=[R1[0:16, :]])
    p.op("dve", lambda e: e.tensor_copy(out=CH[0:16, 1, :], in_=R1[0:16, :]), reads=[R1[0:16, :]], writes=[CH[0:16, 1, :]])
    p.op("dve", lambda e: e.tensor_tensor(out=R1[0:16, :], in0=R1[0:16, :], in1=CH[0:16, 1, :], op=ALU.subtract),
         reads=[R1[0:16, :], CH[0:16, 1, :]], writes=[R1[0:16, :]])
    p.op("dve", lambda e: e.tensor_copy(out=CH[0:16, 2, :], in_=R1[0:16, :]), reads=[R1[0:16, :]], writes=[CH[0:16, 2, :]])
    chd_dep = p.dma("sp", [(chd, CH[0:16, :, :])], "chd")

    for (T, zr, r1, v1) in ((QE, slice(64, 128), slice(96, 99), 1.0), (KE, slice(64, 128), slice(64, 67), -1.0),
                            (QO, slice(0, 64), slice(32, 35), 1.0), (KO, slice(0, 64), slice(0, 3), -1.0)):
        p.op("pool", lambda e: e.memset(T[zr, :], 0.0), writes=[T[zr, :]])
        p.op("pool", lambda e: e.memset(T[r1, :], v1), writes=[T[r1, :]])

    wv = c.din[f"fwv_{l}"].rearrange("(k p) f -> p k f", p=128)
    WV = [load_w(c, wv[:, :, cg * 512:(cg + 1) * 512], [128, 8, 512]) for cg in range(2)]
    VA6 = VA.rearrange("p j s (a d) -> p j s a d", a=3)
    ones_blk = VA6[:, :, :, 1, :]
    for sbk in range(16):
        tt = slice(sbk * 128, (sbk + 1) * 128)
        for cg in range(2):
            bank = next_bank(c, POOL_P)
            for k in range(8):
                lw = c.HT[:, k, tt]
                rh = WV[cg][:, k, :]
                p.op("pe", lambda e: e.matmul(bank, lhsT=lw, rhs=rh, start=(k == 0), stop=(k == 7)),
                     reads=[lw, rh], writes=[bank], inc=(k == 7))
            vo = VA6[:, 4 * cg:4 * cg + 4, sbk, 0:3:2, :]
            bi = bank.rearrange("p (j a d) -> p j a d", j=4, a=2)
            p.op("dve", lambda e: e.tensor_copy(out=vo, in_=bi), reads=[bank], writes=[vo])
    p.op("pool", lambda e: e.memset(ones_blk, 1.0), writes=[ones_blk])

    fq = c.din[f"fqkg_{l}"]
    pti = 0
    for j in range(8):
        W = load_w(c, fq[j].rearrange("p (k c) -> p k c", k=8), [128, 8, 384])
        p.dma("sp", [(QE[64:67, :], chd[2 * j]), (KE[96:99, :], chd[2 * j]),
                     (QO[0:3, :], chd[2 * j + 1]), (KO[32:35, :], chd[2 * j + 1])], "rows", extra_deps=[chd_dep])
        for which, (TE, TO, GV) in enumerate(((QE, QO, c.GQ), (KE, KO, c.GK))):
            for tb in range(4):
                ts_ = slice(tb * 512, (tb + 1) * 512)
                bank = next_bank(c, POOL_P)
                for k in range(8):
                    lw = W[:, k, which * 128:(which + 1) * 128]
                    rh = c.HT[:, k, ts_]
                    p.op("pe", lambda e: e.matmul(bank, lhsT=lw, rhs=rh, start=(k == 0), stop=(k == 7)),
                         reads=[lw, rh], writes=[bank], inc=(k == 7))
                p.op("act", lambda e: e.activation(out=SQb, in_=bank, func=AF.Square), reads=[bank], writes=[SQb])
                bank2 = next_bank(c, POOL_P)
                p.op("pe", lambda e: e.matmul(bank2, lhsT=c.BD, rhs=SQb, start=True, stop=True),
                     reads=[c.BD, SQb], writes=[bank2])
                p.op("act", lambda e: e.activation(out=LNT, in_=bank2, func=AF.Ln, scale=1.0 / DH, bias=c.EPS_RMS),
                     reads=[bank2, c.EPS_RMS], writes=[LNT])
                p.op("act", lambda e: e.activation(out=LNT, in_=LNT, func=AF.Exp, scale=-0.5), reads=[LNT], writes=[LNT])
                for (T, rs) in ((TE, slice(0, 64)), (TO, slice(64, 128))):
                    o = T[rs, ts_]
                    p.op("dve", lambda e: e.scalar_tensor_tensor(out=o, in0=bank[rs, :], scalar=GV[rs, :], in1=LNT[rs, :],
                                                                 op0=ALU.mult, op1=ALU.mult),
                         reads=[bank[rs, :], GV[rs, :], LNT[rs, :]], writes=[o])
        for tb in range(4):
            ts_ = slice(tb * 512, (tb + 1) * 512)
            bank = next_bank(c, POOL_P)
            for k in range(8):
                lw = W[:, k, 256:384]
                rh = c.HT[:, k, ts_]
                p.op("pe", lambda e: e.matmul(bank, lhsT=lw, rhs=rh, start=(k == 0), stop=(k == 7)),
                     reads=[lw, rh], writes=[bank], inc=(k == 7))
            p.op("act", lambda e: e.activation(out=SG[:, ts_], in_=bank, func=AF.Sigmoid), reads=[bank], writes=[SG[:, ts_]])

        OGj = og(j)
        items = []
        for e_ in range(2):
            for g in range(4):
                for sbk in range(4 * g + 4):
                    items.append((e_, g, sbk))
        state = {}

        def emit_qk(it):
            nonlocal pti
            e_, g, sbk = it
            Qt, Kt = (QE, KE) if e_ == 0 else (QO, KO)
            tq0 = g * 512
            c0 = max(0, sbk * 128 - tq0)
            diag = sbk * 128 >= tq0
            lbank = next_bank(c, POOL_L)
            lw = Kt[:, sbk * 128:(sbk + 1) * 128]
            rh = Qt[:, tq0 + c0:tq0 + 512]
            lo = lbank[:, c0:512]
            p.op("pe", lambda e: e.matmul(lo, lhsT=lw, rhs=rh, start=True, stop=not diag),
                 reads=[lw, rh], writes=[lo], inc=not diag)
            if diag:
                lo2 = lbank[:, c0:c0 + 128]
                p.op("pe", lambda e: e.matmul(lo2, lhsT=c.IDENT, rhs=c.MASK, start=False, stop=True),
                     reads=[c.IDENT, c.MASK], writes=[lo2], inc=True)
            pt = PT[pti % 3]
            pti += 1
            po = pt[:, c0:512]
            p.op("act", lambda e: e.activation(out=po, in_=lo, func=AF.Exp), reads=[lo], writes=[po])
            state[it] = (po, c0)

        def emit_pv(it):
            e_, g, sbk = it
            po, c0 = state.pop(it)
            nsb = 4 * g + 4
            if sbk == 0:
                state[("o", e_, g)] = next_bank(c, POOL_O)
            obank = state[("o", e_, g)]
            lw = VA[:, j, sbk, e_ * 64:e_ * 64 + 128]
            oo = obank[:, c0:512]
            p.op("pe", lambda e: e.matmul(oo, lhsT=lw, rhs=po, start=(sbk == 0), stop=(sbk == nsb - 1)),
                 reads=[lw, po], writes=[oo], inc=(sbk == nsb - 1))
            if sbk == nsb - 1:
                ro = slice(0, 64) if e_ == 0 else slice(64, 128)
                rd = slice(64, 128) if e_ == 0 else slice(0, 64)
                tq = slice(g * 512, (g + 1) * 512)
                p.op("act", lambda e: e.activation(out=LND[ro, :], in_=obank[rd, :], func=AF.Ln),
                     reads=[obank[rd, :]], writes=[LND[ro, :]])
                p.op("act", lambda e: e.activation(out=LND[ro, :], in_=LND[ro, :], func=AF.Exp, scale=-1.0),
                     reads=[LND[ro, :]], writes=[LND[ro, :]])
                p.op("dve", lambda e: e.tensor_tensor(out=O1[ro, :], in0=obank[ro, :], in1=LND[ro, :], op=ALU.mult),
                     reads=[obank[ro, :], LND[ro, :]], writes=[O1[ro, :]])
                p.op("dve", lambda e: e.tensor_tensor(out=OGj[ro, tq], in0=O1[ro, :], in1=SG[ro, tq], op=ALU.mult),
                     reads=[O1[ro, :], SG[ro, tq]], writes=[OGj[ro, tq]])
                del state[("o", e_, g)]

        LOOK = 2
        for i in range(min(LOOK, len(items))):
            emit_qk(items[i])
        for i, it in enumerate(items):
            if i + LOOK < len(items):
                emit_qk(items[i + LOOK])
            emit_pv(it)

    wo = c.din[f"fwo_{l}"].rearrange("(k p) f -> p k f", p=128)
    for dh_ in range(2):
        WO = load_w(c, wo[:, :, dh_ * 512:(dh_ + 1) * 512], [128, 8, 512])
        for di in range(4):
            dc = dh_ * 4 + di
            for tb in range(4):
                ts_ = slice(tb * 512, (tb + 1) * 512)
                bank = next_bank(c, POOL_P)
                for kc in range(8):
                    lw = WO[:, kc, di * 128:(di + 1) * 128]
                    rh = og(kc)[:, ts_]
                    p.op("pe", lambda e: e.matmul(bank, lhsT=lw, rhs=rh, start=(kc == 0), stop=(kc == 7)),
                         reads=[lw, rh], writes=[bank], inc=(kc == 7))
                xo = c.XT[:, dc, ts_]
                p.op("dve", lambda e: e.tensor_tensor(out=xo, in0=bank, in1=xo, op=ALU.add), reads=[bank, xo], writes=[xo])


_PROG_CACHE = {}


def _get_prog(sublayers):
    key = tuple(sublayers)
    if key not in _PROG_CACHE:
        _PROG_CACHE[key] = build_program(list(sublayers))
    return _PROG_CACHE[key]


def run_launch(inputs, sublayers, x_cur, cores=NCORES, trace=False):
    nc, c = _get_prog(sublayers)
    in_maps = [prep_inputs(inputs, b, sublayers, x_override=x_cur[b]) for b in range(cores)]
    res = run_bass_kernel_spmd(nc, in_maps, core_ids=list(range(cores)), trace=trace)
    out = np.stack([np.ascontiguousarray(r["outT"].T) for r in res.results], axis=0)
    return out, res


def kernel(**inputs):
    inputs = {k: np.asarray(v) for k, v in inputs.items()}
    x_cur = np.asarray(inputs["x"], dtype=np.float32)
    for sl in LAUNCHES:
        x_cur, _ = run_launch(inputs, sl, x_cur)
    return x_cur.astype(np.float32)
```

```python
import numpy as np
from contextlib import ExitStack

import concourse.bass as bass
import concourse.mybir as mybir
from concourse.bass_utils import run_bass_kernel_spmd

F32 = mybir.dt.float32
BF16 = mybir.dt.bfloat16
AF = mybir.ActivationFunctionType
ALU = mybir.AluOpType

S = 2048
D = 1024
E = 2048
DFF = 4096
H = 16
DH = 64
DEPTH = 4
NCORES = 8
RMS_EPS = 1e-6
LN_EPS = 1e-5
MASK_NEG = -30000.0

FULL = [("gmlp", 0), ("mlp", 0), ("fox", 1), ("mlp", 1), ("gmlp", 2), ("mlp", 2), ("fox", 3), ("mlp", 3)]
LAUNCHES = [FULL]


def _dsize(dt):
    return 4 if dt == F32 else 2


class _Rec:
    __slots__ = ("box", "w", "r")

    def __init__(self, box, w, r):
        self.box = box
        self.w = w
        self.r = r


class Prog:
    def __init__(self, nc, es):
        self.nc = nc
        self.es = es
        self.eng = {"pe": nc.tensor, "act": nc.scalar, "dve": nc.vector, "pool": nc.gpsimd, "sp": nc.sync}
        self.sem = {}
        self.cnt = {}
        for n in ("pe", "act", "dve", "pool"):
            self.sem[n] = es.enter_context(nc.semaphore("s_" + n))
            self.cnt[n] = 0
        self.waited = {n: {} for n in self.eng}
        self.recs = {}
        self.nwaits = 0
        self.ninst = 0

    @staticmethod
    def box(ap):
        t = ap.tensor
        tn = type(t).__name__
        if tn.startswith("DRam"):
            return None
        shp = list(t.shape)
        pitch = _dsize(t.dtype)
        for s_ in shp[1:]:
            pitch *= int(s_)
        es_ = _dsize(ap.dtype)
        offb = int(ap.offset) * es_
        p0 = offb // pitch
        lo = offb % pitch
        pat = ap.ap
        pstep, pcnt = pat[0]
        p1 = p0 + (int(pcnt) if pstep != 0 else 1)
        ext = 1
        for st, c in pat[1:]:
            ext += (int(c) - 1) * abs(int(st))
        return (t.name, p0, p1, lo, lo + ext * es_)

    @staticmethod
    def _ov(a, b):
        return a[1] < b[2] and b[1] < a[2] and a[3] < b[4] and b[3] < a[4]

    @staticmethod
    def _contains(o, i):
        return o[1] <= i[1] and i[2] <= o[2] and o[3] <= i[3] and i[4] <= o[4]

    def _deps(self, stream, reads, writes):
        deps = {}

        def add(dep, kind):
            sk, v = dep
            if sk == stream and stream == "pe":
                return
            if v > deps.get(sk, 0):
                deps[sk] = v

        for ap in reads:
            b = self.box(ap)
            if b is None:
                continue
            for rec in self.recs.get(b[0], ()):
                if rec.w is not None and self._ov(rec.box, b):
                    add(rec.w, "RAW")
        for ap in writes:
            b = self.box(ap)
            if b is None:
                continue
            for rec in self.recs.get(b[0], ()):
                if self._ov(rec.box, b):
                    if rec.w is not None:
                        add(rec.w, "WAW")
                    for sk, v in rec.r.items():
                        add((sk, v), "WAR")
        return deps

    def _commit(self, reads, writes, dep):
        sk, v = dep
        for ap in writes:
            b = self.box(ap)
            if b is None:
                continue
            lst = self.recs.setdefault(b[0], [])
            lst[:] = [r for r in lst if not self._contains(b, r.box)]
            lst.append(_Rec(b, dep, {}))
        for ap in reads:
            b = self.box(ap)
            if b is None:
                continue
            lst = self.recs.setdefault(b[0], [])
            for r in lst:
                if r.box == b:
                    if v > r.r.get(sk, 0):
                        r.r[sk] = v
                    break
            else:
                lst.append(_Rec(b, None, {sk: v}))

    def _wait(self, stream, sk, v):
        w = self.waited[stream]
        if w.get(sk, 0) >= v:
            return
        w[sk] = v
        self.eng[stream].wait_ge(self.sem[sk], v)
        self.nwaits += 1

    def op(self, ename, fn, reads=(), writes=(), inc=True):
        deps = self._deps(ename, reads, writes)
        for sk, v in deps.items():
            self._wait(ename, sk, v)
        ins = fn(self.eng[ename])
        self.ninst += 1
        if inc:
            self.cnt[ename] += 1
            ins.then_inc(self.sem[ename], 1)
            mark = self.cnt[ename]
        else:
            mark = self.cnt[ename] + 1
        self._commit(reads, writes, (ename, mark))
        return ins

    def dsem(self, key):
        if key not in self.sem:
            self.sem[key] = self.es.enter_context(self.nc.semaphore("d_" + key))
            self.cnt[key] = 0
        return self.sem[key]

    def dma(self, q, pairs, key, extra_deps=()):
        sem = self.dsem(key)
        reads = [i for _, i in pairs]
        writes = [o for o, _ in pairs]
        deps = self._deps(q, reads, writes)
        for sk, v in extra_deps:
            if v > deps.get(sk, 0):
                deps[sk] = v
        if self.cnt[key] > 0:
            deps[key] = max(deps.get(key, 0), self.cnt[key])
        for sk, v in deps.items():
            self._wait(q, sk, v)
        for o, i in pairs:
            self.eng[q].dma_start(out=o, in_=i).then_inc(sem, 16)
            self.cnt[key] += 16
            self.ninst += 1
        dep = (key, self.cnt[key])
        self._commit(reads, writes, dep)
        return dep

    def wait_all(self, stream, keys):
        for k in keys:
            if self.cnt.get(k, 0) > 0:
                self._wait(stream, k, self.cnt[k])


class Ctx:
    pass


def _declare_inputs(nc, sublayers):
    d = {}

    def inp(name, shape):
        d[name] = nc.dram_tensor(name, list(shape), F32, kind="ExternalInput").ap()

    inp("xT", [D, S])
    inp("g_mix", [128, DEPTH, 8])
    inp("g_mlp", [128, DEPTH, 8])
    kinds = set(k for k, _ in sublayers)
    layers = sorted(set(l for _, l in sublayers))
    for k, l in sublayers:
        if k == "mlp":
            inp(f"w1_{l}", [D, DFF])
            inp(f"w2r_{l}", [8, 128, 32 * 128])
        elif k == "gmlp":
            inp(f"gwin_{l}", [D, 2 * E])
            inp(f"gwor_{l}", [4, 128, 16 * 256])
            inp(f"glng_{l}", [1, E])
            inp(f"glnb_{l}", [1, E])
            inp(f"gwsT_{l}", [128, 8, 128])
            inp(f"gbs_{l}", [1, 8 * 128])
        elif k == "fox":
            inp(f"fqkg_{l}", [8, 128, 8 * 384])
            inp(f"fwv_{l}", [D, D])
            inp(f"fwf_{l}", [128, 8 * 16])
            inp(f"fwo_{l}", [D, D])
            inp(f"fbf_{l}", [16, 1])
            inp(f"fgq_{l}", [128, 1])
            inp(f"fgk_{l}", [128, 1])
    return d


def prep_inputs(inputs, b, sublayers, x_override=None):
    f = lambda a: np.ascontiguousarray(a, dtype=np.float32)
    m = {}
    xb = inputs["x"][b] if x_override is None else x_override
    m["xT"] = f(np.asarray(xb).T)
    m["g_mix"] = f(np.asarray(inputs["mix_norm_g"]).reshape(DEPTH, 8, 128).transpose(2, 0, 1))
    m["g_mlp"] = f(np.asarray(inputs["mlp_norm_g"]).reshape(DEPTH, 8, 128).transpose(2, 0, 1))
    for k, l in sublayers:
        j = l // 2
        if k == "mlp":
            m[f"w1_{l}"] = f(inputs["mlp_w1"][l])
            w2 = np.asarray(inputs["mlp_w2"][l]).reshape(32, 128, 8, 128)
            m[f"w2r_{l}"] = f(w2.transpose(2, 1, 0, 3).reshape(8, 128, 32 * 128))
        elif k == "gmlp":
            m[f"gwin_{l}"] = f(inputs["gmlp_w_in"][j])
            wo = np.asarray(inputs["gmlp_w_out"][j]).reshape(16, 128, 4, 256)
            m[f"gwor_{l}"] = f(wo.transpose(2, 1, 0, 3).reshape(4, 128, 16 * 256))
            m[f"glng_{l}"] = f(np.asarray(inputs["gmlp_ln_g"][j]).reshape(1, E))
            m[f"glnb_{l}"] = f(np.asarray(inputs["gmlp_ln_b"][j]).reshape(1, E))
            ws = np.asarray(inputs["gmlp_w_s"][j])
            m[f"gwsT_{l}"] = f(ws.transpose(2, 0, 1))
            m[f"gbs_{l}"] = f(np.asarray(inputs["gmlp_b_s"][j]).reshape(1, 8 * 128))
        elif k == "fox":
            wi = np.asarray(inputs["fox_w_in"][j])
            q = wi[:, 0:1024].reshape(8, 128, 8, 128)
            kk = wi[:, 1024:2048].reshape(8, 128, 8, 128)
            gt = wi[:, 3072:4096].reshape(8, 128, 8, 128)
            qkg = np.stack([q, kk, gt], axis=3)
            m[f"fqkg_{l}"] = f(qkg.transpose(2, 1, 0, 3, 4).reshape(8, 128, 8 * 384))
            m[f"fwv_{l}"] = f(wi[:, 2048:3072])
            wf = wi[:, 4096:4112].reshape(8, 128, 16)
            m[f"fwf_{l}"] = f(wf.transpose(1, 0, 2).reshape(128, 8 * 16))
            m[f"fwo_{l}"] = f(inputs["fox_w_out"][j])
            m[f"fbf_{l}"] = f(np.asarray(inputs["fox_b_f"][j]).reshape(16, 1))
            m[f"fgq_{l}"] = f(np.tile(np.asarray(inputs["fox_q_g"][j]), 2).reshape(128, 1))
            m[f"fgk_{l}"] = f(np.tile(np.asarray(inputs["fox_k_g"][j]), 2).reshape(128, 1))
    return m


def build_program(sublayers):
    nc = bass.Bass("TRN2", target_bir_lowering=False)
    es = ExitStack()
    din = _declare_inputs(nc, sublayers)
    outT = nc.dram_tensor("outT", [D, S], F32, kind="ExternalOutput").ap()
    p = Prog(nc, es)
    c = Ctx()
    c.nc, c.p, c.din = nc, p, din

    sb = lambda name, shape, dt: es.enter_context(nc.sbuf_tensor(name, list(shape), dt))
    XT = sb("XT", [128, 8, S], F32)
    HT = sb("HT", [128, 8, S], BF16)
    RING = sb("RING", [128, 4, 4096], BF16)
    CB = sb("CB", [128, 6, 128], BF16)
    CF = sb("CF", [128, 96], F32)
    ARENA_BYTES = 78000
    AR = sb("AR", [128, ARENA_BYTES // 2], BF16)
    PS = es.enter_context(nc.psum_tensor("PS", [128, 8, 512], F32))
    c.XT, c.HT, c.RING, c.AR, c.PS = XT, HT, RING, AR, PS
    c.ONES, c.IDENT, c.ZERO, c.MASK, c.BD, c.SEL = (CB[:, i, :] for i in range(6))
    c.GMIX = CF[:, 0:32].rearrange("p (l k) -> p l k", l=DEPTH)
    c.GMLP = CF[:, 32:64].rearrange("p (l k) -> p l k", l=DEPTH)
    c.EPS_RMS = CF[:, 64:65]
    c.EPS_LN = CF[:, 65:66]
    c.GQ = CF[:, 66:67]
    c.GK = CF[:, 67:68]
    c.NBF = CF[:, 68:69]
    c.CF = CF
    c.bank_i = 0
    c.ring_i = 0

    def arena_bf(off, n):
        assert off % 2 == 0 and off + 2 * n <= ARENA_BYTES, (off, n)
        return AR[:, off // 2: off // 2 + n]

    def arena_f32(off, n):
        assert off % 4 == 0 and off + 4 * n <= ARENA_BYTES, (off, n)
        return AR[:, off // 2: off // 2 + 2 * n].bitcast(F32)

    c.abf, c.af32 = arena_bf, arena_f32

    def ring_f32(slot, off, n):
        return RING[:, slot, off // 2: off // 2 + 2 * n].bitcast(F32)

    c.ring_f32 = ring_f32

    V = "pool"
    p.op(V, lambda e: e.memset(c.ONES, 1.0), writes=[c.ONES])
    p.op(V, lambda e: e.memset(c.ZERO, 0.0), writes=[c.ZERO])
    p.op(V, lambda e: e.affine_select(out=c.IDENT, in_=c.ONES, pattern=[[1, 128]], compare_op=ALU.is_equal,
                                      fill=0.0, base=0, channel_multiplier=-1), reads=[c.ONES], writes=[c.IDENT])
    p.op(V, lambda e: e.affine_select(out=c.MASK, in_=c.ZERO, pattern=[[1, 128]], compare_op=ALU.is_ge,
                                      fill=MASK_NEG, base=0, channel_multiplier=-1), reads=[c.ZERO], writes=[c.MASK])
    p.op(V, lambda e: e.memset(c.BD, 0.0), writes=[c.BD])
    p.op(V, lambda e: e.memset(CB[0:64, 4, 0:64], 1.0), writes=[CB[0:64, 4, 0:64]])
    p.op(V, lambda e: e.memset(CB[64:128, 4, 64:128], 1.0), writes=[CB[64:128, 4, 64:128]])
    p.op(V, lambda e: e.memset(c.SEL, 0.0), writes=[c.SEL])
    p.op(V, lambda e: e.memset(CB[0:1, 5, :], 1.0), writes=[CB[0:1, 5, :]])
    p.op(V, lambda e: e.memset(CB[32:33, 5, :], 1.0), writes=[CB[32:33, 5, :]])
    p.op(V, lambda e: e.memset(c.EPS_RMS, RMS_EPS), writes=[c.EPS_RMS])
    p.op(V, lambda e: e.memset(c.EPS_LN, LN_EPS), writes=[c.EPS_LN])

    xv = din["xT"].rearrange("(k p) t -> p k t", p=128)
    p.dma("sp", [(XT[:, k, :], xv[:, k, :]) for k in range(8)], "xin")
    p.dma("sp", [(CF[:, 0:32], din["g_mix"].rearrange("p l k -> p (l k)")),
                 (CF[:, 32:64], din["g_mlp"].rearrange("p l k -> p (l k)"))], "misc")

    for kind, l in sublayers:
        if kind == "mlp":
            rmsnorm(c, c.GMLP[:, l, :])
            mlp_phase(c, l)
        elif kind == "gmlp":
            rmsnorm(c, c.GMIX[:, l, :])
            gmlp_phase(c, l)
        elif kind == "fox":
            rmsnorm(c, c.GMIX[:, l, :])
            fox_phase(c, l)

    ov = outT.rearrange("(k p) t -> p k t", p=128)
    p.dma("sp", [(ov[:, k, :], XT[:, k, :]) for k in range(8)], "xout")
    p.wait_all("sp", ["xout"])
    for st in ("pe", "act", "dve", "pool", "sp"):
        p.wait_all(st, [k for k in ("pe", "act", "dve", "pool") if k != st])
    c.es = es
    return nc, c


def next_bank(c, pool=None):
    pool = pool or (0, 1, 2, 3, 4, 5, 6, 7)
    b = pool[c.bank_i % len(pool)]
    c.bank_i += 1
    return c.PS[:, b, :]


def next_slot(c, nslots=4):
    s = c.ring_i % nslots
    c.ring_i += 1
    return s


def load_w(c, src, shape):
    s = next_slot(c, c.ring_slots)
    n = shape[1] * shape[2]
    assert n <= 4096
    dst = c.RING[:, s, 0:n].rearrange("p (a b) -> p a b", a=shape[1])
    c.p.dma("pool", [(dst, src)], f"ring{s}")
    return dst


def rmsnorm(c, G):
    p = c.p
    SQ = c.abf(0, 8 * 512).rearrange("p (k t) -> p k t", k=8)
    LNT = c.af32(8192, 512)
    for tb in range(4):
        ts_ = slice(tb * 512, (tb + 1) * 512)
        xin = c.XT[:, :, ts_]
        p.op("act", lambda e: e.activation(out=SQ, in_=xin, func=AF.Square), reads=[xin], writes=[SQ])
        bank = next_bank(c)
        for k in range(8):
            p.op("pe", lambda e: e.matmul(bank, lhsT=c.ONES, rhs=SQ[:, k, :], start=(k == 0), stop=(k == 7)),
                 reads=[c.ONES, SQ[:, k, :]], writes=[bank], inc=(k == 7))
        p.op("act", lambda e: e.activation(out=LNT, in_=bank, func=AF.Ln, scale=1.0 / D, bias=c.EPS_RMS),
             reads=[bank, c.EPS_RMS], writes=[LNT])
        p.op("act", lambda e: e.activation(out=LNT, in_=LNT, func=AF.Exp, scale=-0.5), reads=[LNT], writes=[LNT])
        for k in range(8):
            o = c.HT[:, k, ts_]
            i0 = c.XT[:, k, ts_]
            p.op("dve", lambda e: e.scalar_tensor_tensor(out=o, in0=i0, scalar=G[:, k:k + 1], in1=LNT,
                                                         op0=ALU.mult, op1=ALU.mult),
                 reads=[i0, G[:, k:k + 1], LNT], writes=[o])


def mlp_phase(c, l):
    p = c.p
    c.ring_slots = 4
    HID = c.abf(0, 32 * 1024).rearrange("p (f t) -> p f t", f=32)
    TMP = [c.af32(65536 + 2048 * i, 512) for i in range(2)]
    w1 = c.din[f"w1_{l}"].rearrange("(k p) f -> p k f", p=128)
    w2r = c.din[f"w2r_{l}"]
    ti = 0
    for tbk in range(2):
        t0 = tbk * 1024
        for fg in range(8):
            W = load_w(c, w1[:, :, fg * 512:(fg + 1) * 512], [128, 8, 512])
            for fi in range(4):
                fc = fg * 4 + fi
                for th in range(2):
                    bank = next_bank(c)
                    rsl = slice(t0 + th * 512, t0 + (th + 1) * 512)
                    for k in range(8):
                        lw = W[:, k, fi * 128:(fi + 1) * 128]
                        rh = c.HT[:, k, rsl]
                        p.op("pe", lambda e: e.matmul(bank, lhsT=lw, rhs=rh, start=(k == 0), stop=(k == 7)),
                             reads=[lw, rh], writes=[bank], inc=(k == 7))
                    tmp = TMP[ti % 2]
                    ti += 1
                    p.op("act", lambda e: e.activation(out=tmp, in_=bank, func=AF.Relu), reads=[bank], writes=[tmp])
                    ho = HID[:, fc, th * 512:(th + 1) * 512]
                    p.op("dve", lambda e: e.tensor_tensor(out=ho, in0=tmp, in1=tmp, op=ALU.mult),
                         reads=[tmp], writes=[ho])
        for dc in range(8):
            W2 = load_w(c, w2r[dc], [128, 32, 128])
            for th in range(2):
                bank = next_bank(c)
                for fc in range(32):
                    lw = W2[:, fc, :]
                    rh = HID[:, fc, th * 512:(th + 1) * 512]
                    p.op("pe", lambda e: e.matmul(bank, lhsT=lw, rhs=rh, start=(fc == 0), stop=(fc == 31)),
                         reads=[lw, rh], writes=[bank], inc=(fc == 31))
                xo = c.XT[:, dc, t0 + th * 512:t0 + (th + 1) * 512]
                p.op("dve", lambda e: e.tensor_tensor(out=xo, in0=bank, in1=xo, op=ALU.add),
                     reads=[bank, xo], writes=[xo])


def gmlp_phase(c, l):
    p = c.p
    c.ring_slots = 4
    OFF = 0
    UT = c.abf(0, 16 * 1024).rearrange("p (c t) -> p c t", c=16)
    VG = [c.af32(32768 + 8192 * i, 2048) for i in range(2)]
    VLN = c.abf(49152, 2048)
    LNG = c.af32(53248, 2048)
    LNB = c.af32(61440, 2048)
    WMT = c.abf(69632, 1024).rearrange("p (g t) -> p g t", g=8)
    BS2 = c.abf(71680, 1024).rearrange("p (g t) -> p g t", g=8)
    WSF = c.af32(73728, 1024).rearrange("p (g t) -> p g t", g=8)
    ST = c.CF[:, 69:69 + 24]
    MV = c.CF[:, 93:95]
    RSTD = c.CF[:, 95:96]
    NMR = c.CF[:, 94:95]
    win = c.din[f"gwin_{l}"].rearrange("(k p) f -> p k f", p=128)
    wor = c.din[f"gwor_{l}"]

    p.dma("sp", [(LNG, c.din[f"glng_{l}"].partition_broadcast(128).rearrange("p o e -> p (o e)")),
                 (LNB, c.din[f"glnb_{l}"].partition_broadcast(128).rearrange("p o e -> p (o e)")),
                 (WSF, c.din[f"gwsT_{l}"])], "misc")
    p.op("pool", lambda e: e.affine_select(out=WMT, in_=WSF, pattern=[[0, 8], [1, 128]], compare_op=ALU.is_ge,
                                           fill=0.0, base=0, channel_multiplier=-1), reads=[WSF], writes=[WMT])
    BSF = c.af32(32768, 1024)
    BSH = c.abf(32768 + 4096, 1024)
    p.op("pool", lambda e: e.memset(BS2[0:64], 0.0), writes=[BS2[0:64]])
    p.dma("sp", [(BSF[0:1, :], c.din[f"gbs_{l}"]), (BSF[32:33, :], c.din[f"gbs_{l}"])], "misc")
    bs2f = BS2.rearrange("p g t -> p (g t)")
    p.op("dve", lambda e: e.tensor_copy(out=bs2f[0:1, :], in_=BSF[0:1, :]), reads=[BSF[0:1, :]], writes=[bs2f[0:1, :]])
    p.op("dve", lambda e: e.tensor_copy(out=BSH[32:33, :], in_=BSF[32:33, :]), reads=[BSF[32:33, :]],
         writes=[BSH[32:33, :]])
    p.op("dve", lambda e: e.tensor_tensor(out=bs2f[32:33, :], in0=BSF[32:33, :], in1=BSH[32:33, :], op=ALU.subtract),
         reads=[BSF[32:33, :], BSH[32:33, :]], writes=[bs2f[32:33, :]])

    import os as _os
    _stop = _os.environ.get("GMLP_STOP", "")
    if _stop == "setup":
        return
    vi = 0
    for tbk in range(2):
        t0 = tbk * 1024
        for ug in range(4):
            W = load_w(c, win[:, :, ug * 512:(ug + 1) * 512], [128, 8, 512])
            for ci in range(4):
                cc = ug * 4 + ci
                for th in range(2):
                    bank = next_bank(c)
                    rsl = slice(t0 + th * 512, t0 + (th + 1) * 512)
                    for k in range(8):
                        lw = W[:, k, ci * 128:(ci + 1) * 128]
                        rh = c.HT[:, k, rsl]
                        p.op("pe", lambda e: e.matmul(bank, lhsT=lw, rhs=rh, start=(k == 0), stop=(k == 7)),
                             reads=[lw, rh], writes=[bank], inc=(k == 7))
                    uo = UT[:, cc, th * 512:(th + 1) * 512]
                    p.op("act", lambda e: e.activation(out=uo, in_=bank, func=AF.Gelu_apprx_tanh),
                         reads=[bank], writes=[uo])
        if _stop == "U" or (_stop == "U2" and tbk == 1):
            return
        WV = [load_w(c, win[:, :, E + cg * 512:E + (cg + 1) * 512], [128, 8, 512]) for cg in range(4)]
        for n in range(8):
            tt = slice(t0 + n * 128, t0 + (n + 1) * 128)
            vg = VG[vi % 2]
            vi += 1
            for cg in range(4):
                bank = next_bank(c)
                for k in range(8):
                    lw = c.HT[:, k, tt]
                    rh = WV[cg][:, k, :]
                    p.op("pe", lambda e: e.matmul(bank, lhsT=lw, rhs=rh, start=(k == 0), stop=(k == 7)),
                         reads=[lw, rh], writes=[bank], inc=(k == 7))
                vo = vg[:, cg * 512:(cg + 1) * 512]
                p.op("act", lambda e: e.activation(out=vo, in_=bank, func=AF.Gelu_apprx_tanh),
                     reads=[bank], writes=[vo])
                so = ST[:, cg * 6:(cg + 1) * 6]
                p.op("dve", lambda e: e.bn_stats(out=so, in_=vo), reads=[vo], writes=[so])
            p.op("dve", lambda e: e.bn_aggr(out=MV, in_=ST), reads=[ST], writes=[MV])
            p.op("act", lambda e: e.activation(out=RSTD, in_=MV[:, 1:2], func=AF.Ln, bias=c.EPS_LN, scale=1.0),
                 reads=[MV[:, 1:2], c.EPS_LN], writes=[RSTD])
            p.op("act", lambda e: e.activation(out=RSTD, in_=RSTD, func=AF.Exp, scale=-0.5), reads=[RSTD], writes=[RSTD])
            p.op("dve", lambda e: e.scalar_tensor_tensor(out=NMR, in0=MV[:, 0:1], scalar=-1.0, in1=RSTD,
                                                         op0=ALU.mult, op1=ALU.mult),
                 reads=[MV[:, 0:1], RSTD], writes=[NMR])
            p.op("act", lambda e: e.activation(out=vg, in_=vg, func=AF.Identity, scale=RSTD, bias=NMR),
                 reads=[vg, RSTD, NMR], writes=[vg])
            p.op("dve", lambda e: e.tensor_tensor(out=vg, in0=vg, in1=LNG, op=ALU.mult), reads=[vg, LNG], writes=[vg])
            p.op("pool", lambda e: e.tensor_tensor(out=VLN, in0=vg, in1=LNB, op=ALU.add), reads=[vg, LNB], writes=[VLN])
            if _stop == "V" or (_stop == "V2" and tbk == 1):
                continue
            for cq in range(4):
                bank = next_bank(c)
                for ci in range(4):
                    cc = cq * 4 + ci
                    g = cc // 2
                    bo = bank[:, ci * 128:(ci + 1) * 128]
                    lw = VLN[:, cc * 128:(cc + 1) * 128]
                    rh = WMT[:, g, :]
                    p.op("pe", lambda e: e.matmul(bo, lhsT=lw, rhs=rh, start=True, stop=False),
                         reads=[lw, rh], writes=[bo], inc=False)
                    lw2 = c.SEL[0:64, :]
                    rh2 = BS2[0:64, g, :]
                    p.op("pe", lambda e: e.matmul(bo, lhsT=lw2, rhs=rh2, start=False, stop=True),
                         reads=[lw2, rh2], writes=[bo], inc=(ci == 3))
                uo = UT[:, cq * 4:(cq + 1) * 4, n * 128:(n + 1) * 128]
                bi = bank.rearrange("p (a b) -> p a b", a=4)
                p.op("dve", lambda e: e.tensor_tensor(out=uo, in0=bi, in1=uo, op=ALU.mult), reads=[bank, uo], writes=[uo])
        if _stop in ("V", "S") or (_stop in ("V2", "S2") and tbk == 1):
            continue
        _o2 = _os.environ.get("GMLP_O2", "all") if tbk == 1 else "all"
        for dcp in range(4):
            WO = load_w(c, wor[dcp], [128, 16, 256])
            if _o2 == "loads":
                continue
            for di in range(2):
                dc = dcp * 2 + di
                for th in range(2):
                    bank = next_bank(c)
                    for kc in range(16):
                        lw = WO[:, kc, di * 128:(di + 1) * 128]
                        rh = UT[:, kc, th * 512:(th + 1) * 512]
                        p.op("pe", lambda e: e.matmul(bank, lhsT=lw, rhs=rh, start=(kc == 0), stop=(kc == 15)),
                             reads=[lw, rh], writes=[bank], inc=(kc == 15))
                    if _o2 == "mm":
                        continue
                    xo = c.XT[:, dc, t0 + th * 512:t0 + (th + 1) * 512]
                    p.op("dve", lambda e: e.tensor_tensor(out=xo, in0=bank, in1=xo, op=ALU.add),
                         reads=[bank, xo], writes=[xo])
        if _stop == "O1":
            return


def fox_phase(c, l):
    p = c.p
    nc = c.nc
    c.ring_slots = 3
    POOL_O = (0, 1)
    POOL_L = (2, 3, 4)
    POOL_P = (5, 6, 7)
    VA = c.abf(0, 8 * 16 * 192).rearrange("p (j s c) -> p j s c", j=8, s=16)
    QK = [c.abf(49152 + 4096 * i, 2048) for i in range(4)]
    QE, QO, KE, KO = QK
    SG = c.abf(65536, 2048)
    PT = [c.abf(69632 + 1024 * i, 512) for i in range(3)]
    OG0 = c.abf(72704, 2048)
    SQb = c.RING[:, 3, 0:512]
    LNT = c.ring_f32(3, 1024, 512)
    LND = c.ring_f32(3, 3072, 512)
    O1 = c.ring_f32(3, 5120, 512)
    LF = c.af32(0, 2048)
    CS = c.af32(8192, 2048)
    ONF = c.af32(16384, 2048)
    R1 = c.af32(24576, 2048)
    CH = c.abf(32768, 3 * 2048).rearrange("p (a t) -> p a t", a=3)
    chd = nc.dram_tensor(f"chd_{l}", [16, 3, 2048], BF16, kind="Internal").ap()

    def og(j):
        if j == 0:
            return OG0
        return VA[:, j - 1].rearrange("p s c -> p (s c)")[:, 0:2048]

    p.dma("sp", [(c.GQ, c.din[f"fgq_{l}"]), (c.GK, c.din[f"fgk_{l}"]), (c.NBF[0:16, :], c.din[f"fbf_{l}"])], "misc")
    p.op("dve", lambda e: e.tensor_scalar(out=c.GQ, in0=c.GQ, scalar1=DH ** -0.5, scalar2=None, op0=ALU.mult),
         reads=[c.GQ], writes=[c.GQ])
    p.op("dve", lambda e: e.tensor_scalar(out=c.NBF[0:16, :], in0=c.NBF[0:16, :], scalar1=-1.0, scalar2=None,
                                          op0=ALU.mult), reads=[c.NBF[0:16, :]], writes=[c.NBF[0:16, :]])

    WF = load_w(c, c.din[f"fwf_{l}"].rearrange("p (k h) -> p k h", k=8), [128, 8, 16])
    p.op("pool", lambda e: e.memset(ONF[0:16, :], 1.0), writes=[ONF[0:16, :]])
    for tb in range(4):
        ts_ = slice(tb * 512, (tb + 1) * 512)
        bank = next_bank(c, POOL_P)
        for k in range(8):
            lw = WF[:, k, :]
            rh = c.HT[:, k, ts_]
            p.op("pe", lambda e: e.matmul(bank[0:16, :], lhsT=lw, rhs=rh, start=(k == 0), stop=(k == 7)),
                 reads=[lw, rh], writes=[bank[0:16, :]], inc=(k == 7))
        lo = LF[0:16, ts_]
        p.op("act", lambda e: e.activation(out=lo, in_=bank[0:16, :], func=AF.Exp, scale=-1.0, bias=c.NBF[0:16, :]),
             reads=[bank[0:16, :], c.NBF[0:16, :]], writes=[lo])
        p.op("act", lambda e: e.activation(out=lo, in_=lo, func=AF.Ln, scale=1.0, bias=1.0), reads=[lo], writes=[lo])
    p.op("dve", lambda e: e.tensor_tensor_scan(out=CS[0:16, :], data0=ONF[0:16, :], data1=LF[0:16, :], initial=0.0,
                                               op0=ALU.mult, op1=ALU.add),
         reads=[ONF[0:16, :], LF[0:16, :]], writes=[CS[0:16, :]])
    p.op("dve", lambda e: e.tensor_copy(out=CH[0:16, 0, :], in_=CS[0:16, :]), reads=[CS[0:16, :]], writes=[CH[0:16, 0, :]])
    p.op("dve", lambda e: e.tensor_tensor(out=R1[0:16, :], in0=CS[0:16, :], in1=CH[0:16, 0, :], op=ALU.subtract),
         reads=[CS[0:16, :], CH[0:16, 0, :]], writes=[R1[0:16, :]])
    p.op("dve", lambda e: e.tensor_copy(out=CH[0:16, 1, :], in_=R1[0:16, :]), reads=[R1[0:16, :]], writes=[CH[0:16, 1, :]])
    p.op("dve", lambda e: e.tensor_tensor(out=R1[0:16, :], in0=R1[0:16, :], in1=CH[0:16, 1, :], op=ALU.subtract),
         reads=[R1[0:16, :], CH[0:16, 1, :]], writes=[R1[0:16, :]])
    p.op("dve", lambda e: e.tensor_copy(out=CH[0:16, 2, :], in_=R1[0:16, :]), reads=[R1[0:16, :]], writes=[CH[0:16, 2, :]])
    chd_dep = p.dma("sp", [(chd, CH[0:16, :, :])], "chd")

    for (T, zr, r1, v1) in ((QE, slice(64, 128), slice(96, 99), 1.0), (KE, slice(64, 128), slice(64, 67), -1.0),
                            (QO, slice(0, 64), slice(32, 35), 1.0), (KO, slice(0, 64), slice(0, 3), -1.0)):
        p.op("pool", lambda e: e.memset(T[zr, :], 0.0), writes=[T[zr, :]])
        p.op("pool", lambda e: e.memset(T[r1, :], v1), writes=[T[r1, :]])

    wv = c.din[f"fwv_{l}"].rearrange("(k p) f -> p k f", p=128)
    WV = [load_w(c, wv[:, :, cg * 512:(cg + 1) * 512], [128, 8, 512]) for cg in range(2)]
    VA6 = VA.rearrange("p j s (a d) -> p j s a d", a=3)
    ones_blk = VA6[:, :, :, 1, :]
    for sbk in range(16):
        tt = slice(sbk * 128, (sbk + 1) * 128)
        for cg in range(2):
            bank = next_bank(c, POOL_P)
            for k in range(8):
                lw = c.HT[:, k, tt]
                rh = WV[cg][:, k, :]
                p.op("pe", lambda e: e.matmul(bank, lhsT=lw, rhs=rh, start=(k == 0), stop=(k == 7)),
                     reads=[lw, rh], writes=[bank], inc=(k == 7))
            vo = VA6[:, 4 * cg:4 * cg + 4, sbk, 0:3:2, :]
            bi = bank.rearrange("p (j a d) -> p j a d", j=4, a=2)
            p.op("dve", lambda e: e.tensor_copy(out=vo, in_=bi), reads=[bank], writes=[vo])
    p.op("pool", lambda e: e.memset(ones_blk, 1.0), writes=[ones_blk])

    fq = c.din[f"fqkg_{l}"]
    pti = 0
    for j in range(8):
        W = load_w(c, fq[j].rearrange("p (k c) -> p k c", k=8), [128, 8, 384])
        p.dma("sp", [(QE[64:67, :], chd[2 * j]), (KE[96:99, :], chd[2 * j]),
                     (QO[0:3, :], chd[2 * j + 1]), (KO[32:35, :], chd[2 * j + 1])], "rows", extra_deps=[chd_dep])
        for which, (TE, TO, GV) in enumerate(((QE, QO, c.GQ), (KE, KO, c.GK))):
            for tb in range(4):
                ts_ = slice(tb * 512, (tb + 1) * 512)
                bank = next_bank(c, POOL_P)
                for k in range(8):
                    lw = W[:, k, which * 128:(which + 1) * 128]
                    rh = c.HT[:, k, ts_]
                    p.op("pe", lambda e: e.matmul(bank, lhsT=lw, rhs=rh, start=(k == 0), stop=(k == 7)),
                         reads=[lw, rh], writes=[bank], inc=(k == 7))
                p.op("act", lambda e: e.activation(out=SQb, in_=bank, func=AF.Square), reads=[bank], writes=[SQb])
                bank2 = next_bank(c, POOL_P)
                p.op("pe", lambda e: e.matmul(bank2, lhsT=c.BD, rhs=SQb, start=True, stop=True),
                     reads=[c.BD, SQb], writes=[bank2])
                p.op("act", lambda e: e.activation(out=LNT, in_=bank2, func=AF.Ln, scale=1.0 / DH, bias=c.EPS_RMS),
                     reads=[bank2, c.EPS_RMS], writes=[LNT])
                p.op("act", lambda e: e.activation(out=LNT, in_=LNT, func=AF.Exp, scale=-0.5), reads=[LNT], writes=[LNT])
                for (T, rs) in ((TE, slice(0, 64)), (TO, slice(64, 128))):
                    o = T[rs, ts_]
                    p.op("dve", lambda e: e.scalar_tensor_tensor(out=o, in0=bank[rs, :], scalar=GV[rs, :], in1=LNT[rs, :],
                                                                 op0=ALU.mult, op1=ALU.mult),
                         reads=[bank[rs, :], GV[rs, :], LNT[rs, :]], writes=[o])
        for tb in range(4):
            ts_ = slice(tb * 512, (tb + 1) * 512)
            bank = next_bank(c, POOL_P)
            for k in range(8):
                lw = W[:, k, 256:384]
                rh = c.HT[:, k, ts_]
                p.op("pe", lambda e: e.matmul(bank, lhsT=lw, rhs=rh, start=(k == 0), stop=(k == 7)),
                     reads=[lw, rh], writes=[bank], inc=(k == 7))
            p.op("act", lambda e: e.activation(out=SG[:, ts_], in_=bank, func=AF.Sigmoid), reads=[bank], writes=[SG[:, ts_]])

        OGj = og(j)
        items = []
        for e_ in range(2):
            for g in range(4):
                for sbk in range(4 * g + 4):
                    items.append((e_, g, sbk))
        state = {}

        def emit_qk(it):
            nonlocal pti
            e_, g, sbk = it
            Qt, Kt = (QE, KE) if e_ == 0 else (QO, KO)
            tq0 = g * 512
            c0 = max(0, sbk * 128 - tq0)
            diag = sbk * 128 >= tq0
            lbank = next_bank(c, POOL_L)
            lw = Kt[:, sbk * 128:(sbk + 1) * 128]
            rh = Qt[:, tq0 + c0:tq0 + 512]
            lo = lbank[:, c0:512]
            p.op("pe", lambda e: e.matmul(lo, lhsT=lw, rhs=rh, start=True, stop=not diag),
                 reads=[lw, rh], writes=[lo], inc=not diag)
            if diag:
                lo2 = lbank[:, c0:c0 + 128]
                p.op("pe", lambda e: e.matmul(lo2, lhsT=c.IDENT, rhs=c.MASK, start=False, stop=True),
                     reads=[c.IDENT, c.MASK], writes=[lo2], inc=True)
            pt = PT[pti % 3]
            pti += 1
            po = pt[:, c0:512]
            p.op("act", lambda e: e.activation(out=po, in_=lo, func=AF.Exp), reads=[lo], writes=[po])
            state[it] = (po, c0)

        def emit_pv(it):
            e_, g, sbk = it
            po, c0 = state.pop(it)
            nsb = 4 * g + 4
            if sbk == 0:
                state[("o", e_, g)] = next_bank(c, POOL_O)
            obank = state[("o", e_, g)]
            lw = VA[:, j, sbk, e_ * 64:e_ * 64 + 128]
            oo = obank[:, c0:512]
            p.op("pe", lambda e: e.matmul(oo, lhsT=lw, rhs=po, start=(sbk == 0), stop=(sbk == nsb - 1)),
                 reads=[lw, po], writes=[oo], inc=(sbk == nsb - 1))
            if sbk == nsb - 1:
                ro = slice(0, 64) if e_ == 0 else slice(64, 128)
                rd = slice(64, 128) if e_ == 0 else slice(0, 64)
                tq = slice(g * 512, (g + 1) * 512)
                p.op("act", lambda e: e.activation(out=LND[ro, :], in_=obank[rd, :], func=AF.Ln),
                     reads=[obank[rd, :]], writes=[LND[ro, :]])
                p.op("act", lambda e: e.activation(out=LND[ro, :], in_=LND[ro, :], func=AF.Exp, scale=-1.0),
                     reads=[LND[ro, :]], writes=[LND[ro, :]])
                p.op("dve", lambda e: e.tensor_tensor(out=O1[ro, :], in0=obank[ro, :], in1=LND[ro, :], op=ALU.mult),
                     reads=[obank[ro, :], LND[ro, :]], writes=[O1[ro, :]])
                p.op("dve", lambda e: e.tensor_tensor(out=OGj[ro, tq], in0=O1[ro, :], in1=SG[ro, tq], op=ALU.mult),
                     reads=[O1[ro, :], SG[ro, tq]], writes=[OGj[ro, tq]])
                del state[("o", e_, g)]

        LOOK = 2
        for i in range(min(LOOK, len(items))):
            emit_qk(items[i])
        for i, it in enumerate(items):
            if i + LOOK < len(items):
                emit_qk(items[i + LOOK])
            emit_pv(it)

    wo = c.din[f"fwo_{l}"].rearrange("(k p) f -> p k f", p=128)
    for dh_ in range(2):
        WO = load_w(c, wo[:, :, dh_ * 512:(dh_ + 1) * 512], [128, 8, 512])
        for di in range(4):
            dc = dh_ * 4 + di
            for tb in range(4):
                ts_ = slice(tb * 512, (tb + 1) * 512)
                bank = next_bank(c, POOL_P)
                for kc in range(8):
                    lw = WO[:, kc, di * 128:(di + 1) * 128]
                    rh = og(kc)[:, ts_]
                    p.op("pe", lambda e: e.matmul(bank, lhsT=lw, rhs=rh, start=(kc == 0), stop=(kc == 7)),
                         reads=[lw, rh], writes=[bank], inc=(kc == 7))
                xo = c.XT[:, dc, ts_]
                p.op("dve", lambda e: e.tensor_tensor(out=xo, in0=bank, in1=xo, op=ALU.add), reads=[bank, xo], writes=[xo])


_PROG_CACHE = {}


def _get_prog(sublayers):
    key = tuple(sublayers)
    if key not in _PROG_CACHE:
        _PROG_CACHE[key] = build_program(list(sublayers))
    return _PROG_CACHE[key]


def run_launch(inputs, sublayers, x_cur, cores=NCORES, trace=False):
    nc, c = _get_prog(sublayers)
    in_maps = [prep_inputs(inputs, b, sublayers, x_override=x_cur[b]) for b in range(cores)]
    res = run_bass_kernel_spmd(nc, in_maps, core_ids=list(range(cores)), trace=trace)
    out = np.stack([np.ascontiguousarray(r["outT"].T) for r in res.results], axis=0)
    return out, res


def kernel(**inputs):
    inputs = {k: np.asarray(v) for k, v in inputs.items()}
    x_cur = np.asarray(inputs["x"], dtype=np.float32)
    for sl in LAUNCHES:
        x_cur, _ = run_launch(inputs, sl, x_cur)
    return x_cur.astype(np.float32)
```

```python
import numpy as np
from contextlib import ExitStack

import concourse.bass as bass
import concourse.mybir as mybir
from concourse.bass_utils import run_bass_kernel_spmd

F32 = mybir.dt.float32
BF16 = mybir.dt.bfloat16
AF = mybir.ActivationFunctionType
ALU = mybir.AluOpType

S = 2048
D = 1024
E = 2048
DFF = 4096
H = 16
DH = 64
DEPTH = 4
NCORES = 8
RMS_EPS = 1e-6
LN_EPS = 1e-5
MASK_NEG = -30000.0

FULL = [("gmlp", 0), ("mlp", 0), ("fox", 1), ("mlp", 1), ("gmlp", 2), ("mlp", 2), ("fox", 3), ("mlp", 3)]
LAUNCHES = [FULL]


def _dsize(dt):
    return 4 if dt == F32 else 2


class _Rec:
    __slots__ = ("box", "w", "r")

    def __init__(self, box, w, r):
        self.box = box
        self.w = w
        self.r = r


class Prog:
    def __init__(self, nc, es):
        self.nc = nc
        self.es = es
        self.eng = {"pe": nc.tensor, "act": nc.scalar, "dve": nc.vector, "pool": nc.gpsimd, "sp": nc.sync}
        self.sem = {}
        self.cnt = {}
        for n in ("pe", "act", "dve", "pool"):
            self.sem[n] = es.enter_context(nc.semaphore("s_" + n))
            self.cnt[n] = 0
        self.waited = {n: {} for n in self.eng}
        self.recs = {}
        self.nwaits = 0
        self.ninst = 0

    @staticmethod
    def box(ap):
        t = ap.tensor
        tn = type(t).__name__
        if tn.startswith("DRam"):
            return None
        shp = list(t.shape)
        pitch = _dsize(t.dtype)
        for s_ in shp[1:]:
            pitch *= int(s_)
        es_ = _dsize(ap.dtype)
        offb = int(ap.offset) * es_
        p0 = offb // pitch
        lo = offb % pitch
        pat = ap.ap
        pstep, pcnt = pat[0]
        p1 = p0 + (int(pcnt) if pstep != 0 else 1)
        ext = 1
        for st, c in pat[1:]:
            ext += (int(c) - 1) * abs(int(st))
        return (t.name, p0, p1, lo, lo + ext * es_)

    @staticmethod
    def _ov(a, b):
        return a[1] < b[2] and b[1] < a[2] and a[3] < b[4] and b[3] < a[4]

    @staticmethod
    def _contains(o, i):
        return o[1] <= i[1] and i[2] <= o[2] and o[3] <= i[3] and i[4] <= o[4]

    def _deps(self, stream, reads, writes):
        deps = {}

        def add(dep, kind):
            sk, v = dep
            if sk == stream and stream == "pe":
                return
            if v > deps.get(sk, 0):
                deps[sk] = v

        for ap in reads:
            b = self.box(ap)
            if b is None:
                continue
            for rec in self.recs.get(b[0], ()):
                if rec.w is not None and self._ov(rec.box, b):
                    add(rec.w, "RAW")
        for ap in writes:
            b = self.box(ap)
            if b is None:
                continue
            for rec in self.recs.get(b[0], ()):
                if self._ov(rec.box, b):
                    if rec.w is not None:
                        add(rec.w, "WAW")
                    for sk, v in rec.r.items():
                        add((sk, v), "WAR")
        return deps

    def _commit(self, reads, writes, dep):
        sk, v = dep
        for ap in writes:
            b = self.box(ap)
            if b is None:
                continue
            lst = self.recs.setdefault(b[0], [])
            lst[:] = [r for r in lst if not self._contains(b, r.box)]
            lst.append(_Rec(b, dep, {}))
        for ap in reads:
            b = self.box(ap)
            if b is None:
                continue
            lst = self.recs.setdefault(b[0], [])
            for r in lst:
                if r.box == b:
                    if v > r.r.get(sk, 0):
                        r.r[sk] = v
                    break
            else:
                lst.append(_Rec(b, None, {sk: v}))

    def _wait(self, stream, sk, v):
        w = self.waited[stream]
        if w.get(sk, 0) >= v:
            return
        w[sk] = v
        self.eng[stream].wait_ge(self.sem[sk], v)
        self.nwaits += 1

    def op(self, ename, fn, reads=(), writes=(), inc=True):
        deps = self._deps(ename, reads, writes)
        for sk, v in deps.items():
            self._wait(ename, sk, v)
        ins = fn(self.eng[ename])
        self.ninst += 1
        if inc:
            self.cnt[ename] += 1
            ins.then_inc(self.sem[ename], 1)
            mark = self.cnt[ename]
        else:
            mark = self.cnt[ename] + 1
        self._commit(reads, writes, (ename, mark))
        return ins

    def dsem(self, key):
        if key not in self.sem:
            self.sem[key] = self.es.enter_context(self.nc.semaphore("d_" + key))
            self.cnt[key] = 0
        return self.sem[key]

    def dma(self, q, pairs, key, extra_deps=()):
        sem = self.dsem(key)
        reads = [i for _, i in pairs]
        writes = [o for o, _ in pairs]
        deps = self._deps(q, reads, writes)
        for sk, v in extra_deps:
            if v > deps.get(sk, 0):
                deps[sk] = v
        if self.cnt[key] > 0:
            deps[key] = max(deps.get(key, 0), self.cnt[key])
        for sk, v in deps.items():
            self._wait(q, sk, v)
        for o, i in pairs:
            self.eng[q].dma_start(out=o, in_=i).then_inc(sem, 16)
            self.cnt[key] += 16
            self.ninst += 1
        dep = (key, self.cnt[key])
        self._commit(reads, writes, dep)
        return dep

    def wait_all(self, stream, keys):
        for k in keys:
            if self.cnt.get(k, 0) > 0:
                self._wait(stream, k, self.cnt[k])


class Ctx:
    pass


def _declare_inputs(nc, sublayers):
    d = {}

    def inp(name, shape):
        d[name] = nc.dram_tensor(name, list(shape), F32, kind="ExternalInput").ap()

    inp("xT", [D, S])
    inp("g_mix", [128, DEPTH, 8])
    inp("g_mlp", [128, DEPTH, 8])
    kinds = set(k for k, _ in sublayers)
    layers = sorted(set(l for _, l in sublayers))
    for k, l in sublayers:
        if k == "mlp":
            inp(f"w1_{l}", [D, DFF])
            inp(f"w2r_{l}", [8, 128, 32 * 128])
        elif k == "gmlp":
            inp(f"gwin_{l}", [D, 2 * E])
            inp(f"gwor_{l}", [4, 128, 16 * 256])
            inp(f"glng_{l}", [1, E])
            inp(f"glnb_{l}", [1, E])
            inp(f"gwsT_{l}", [128, 8, 128])
            inp(f"gbs_{l}", [1, 8 * 128])
        elif k == "fox":
            inp(f"fqkg_{l}", [8, 128, 8 * 384])
            inp(f"fwv_{l}", [D, D])
            inp(f"fwf_{l}", [128, 8 * 16])
            inp(f"fwo_{l}", [D, D])
            inp(f"fbf_{l}", [16, 1])
            inp(f"fgq_{l}", [128, 1])
            inp(f"fgk_{l}", [128, 1])
    return d


def prep_inputs(inputs, b, sublayers, x_override=None):
    f = lambda a: np.ascontiguousarray(a, dtype=np.float32)
    m = {}
    xb = inputs["x"][b] if x_override is None else x_override
    m["xT"] = f(np.asarray(xb).T)
    m["g_mix"] = f(np.asarray(inputs["mix_norm_g"]).reshape(DEPTH, 8, 128).transpose(2, 0, 1))
    m["g_mlp"] = f(np.asarray(inputs["mlp_norm_g"]).reshape(DEPTH, 8, 128).transpose(2, 0, 1))
    for k, l in sublayers:
        j = l // 2
        if k == "mlp":
            m[f"w1_{l}"] = f(inputs["mlp_w1"][l])
            w2 = np.asarray(inputs["mlp_w2"][l]).reshape(32, 128, 8, 128)
            m[f"w2r_{l}"] = f(w2.transpose(2, 1, 0, 3).reshape(8, 128, 32 * 128))
        elif k == "gmlp":
            m[f"gwin_{l}"] = f(inputs["gmlp_w_in"][j])
            wo = np.asarray(inputs["gmlp_w_out"][j]).reshape(16, 128, 4, 256)
            m[f"gwor_{l}"] = f(wo.transpose(2, 1, 0, 3).reshape(4, 128, 16 * 256))
            m[f"glng_{l}"] = f(np.asarray(inputs["gmlp_ln_g"][j]).reshape(1, E))
            m[f"glnb_{l}"] = f(np.asarray(inputs["gmlp_ln_b"][j]).reshape(1, E))
            ws = np.asarray(inputs["gmlp_w_s"][j])
            m[f"gwsT_{l}"] = f(ws.transpose(2, 0, 1))
            m[f"gbs_{l}"] = f(np.asarray(inputs["gmlp_b_s"][j]).reshape(1, 8 * 128))
        elif k == "fox":
            wi = np.asarray(inputs["fox_w_in"][j])
            q = wi[:, 0:1024].reshape(8, 128, 8, 128)
            kk = wi[:, 1024:2048].reshape(8, 128, 8, 128)
            gt = wi[:, 3072:4096].reshape(8, 128, 8, 128)
            qkg = np.stack([q, kk, gt], axis=3)
            m[f"fqkg_{l}"] = f(qkg.transpose(2, 1, 0, 3, 4).reshape(8, 128, 8 * 384))
            m[f"fwv_{l}"] = f(wi[:, 2048:3072])
            wf = wi[:, 4096:4112].reshape(8, 128, 16)
            m[f"fwf_{l}"] = f(wf.transpose(1, 0, 2).reshape(128, 8 * 16))
            m[f"fwo_{l}"] = f(inputs["fox_w_out"][j])
            m[f"fbf_{l}"] = f(np.asarray(inputs["fox_b_f"][j]).reshape(16, 1))
            m[f"fgq_{l}"] = f(np.tile(np.asarray(inputs["fox_q_g"][j]), 2).reshape(128, 1))
            m[f"fgk_{l}"] = f(np.tile(np.asarray(inputs["fox_k_g"][j]), 2).reshape(128, 1))
    return m


def build_program(sublayers):
    nc = bass.Bass("TRN2", target_bir_lowering=False)
    es = ExitStack()
    din = _declare_inputs(nc, sublayers)
    outT = nc.dram_tensor("outT", [D, S], F32, kind="ExternalOutput").ap()
    p = Prog(nc, es)
    c = Ctx()
    c.nc, c.p, c.din = nc, p, din

    sb = lambda name, shape, dt: es.enter_context(nc.sbuf_tensor(name, list(shape), dt))
    XT = sb("XT", [128, 8, S], F32)
    HT = sb("HT", [128, 8, S], BF16)
    RING = sb("RING", [128, 4, 4096], BF16)
    CB = sb("CB", [128, 6, 128], BF16)
    CF = sb("CF", [128, 96], F32)
    ARENA_BYTES = 78000
    AR = sb("AR", [128, ARENA_BYTES // 2], BF16)
    PS = es.enter_context(nc.psum_tensor("PS", [128, 8, 512], F32))
    c.XT, c.HT, c.RING, c.AR, c.PS = XT, HT, RING, AR, PS
    c.ONES, c.IDENT, c.ZERO, c.MASK, c.BD, c.SEL = (CB[:, i, :] for i in range(6))
    c.GMIX = CF[:, 0:32].rearrange("p (l k) -> p l k", l=DEPTH)
    c.GMLP = CF[:, 32:64].rearrange("p (l k) -> p l k", l=DEPTH)
    c.EPS_RMS = CF[:, 64:65]
    c.EPS_LN = CF[:, 65:66]
    c.GQ = CF[:, 66:67]
    c.GK = CF[:, 67:68]
    c.NBF = CF[:, 68:69]
    c.CF = CF
    c.bank_i = 0
    c.ring_i = 0

    def arena_bf(off, n):
        assert off % 2 == 0 and off + 2 * n <= ARENA_BYTES, (off, n)
        return AR[:, off // 2: off // 2 + n]

    def arena_f32(off, n):
        assert off % 4 == 0 and off + 4 * n <= ARENA_BYTES, (off, n)
        return AR[:, off // 2: off // 2 + 2 * n].bitcast(F32)

    c.abf, c.af32 = arena_bf, arena_f32

    def ring_f32(slot, off, n):
        return RING[:, slot, off // 2: off // 2 + 2 * n].bitcast(F32)

    c.ring_f32 = ring_f32

    V = "pool"
    p.op(V, lambda e: e.memset(c.ONES, 1.0), writes=[c.ONES])
    p.op(V, lambda e: e.memset(c.ZERO, 0.0), writes=[c.ZERO])
    p.op(V, lambda e: e.affine_select(out=c.IDENT, in_=c.ONES, pattern=[[1, 128]], compare_op=ALU.is_equal,
                                      fill=0.0, base=0, channel_multiplier=-1), reads=[c.ONES], writes=[c.IDENT])
    p.op(V, lambda e: e.affine_select(out=c.MASK, in_=c.ZERO, pattern=[[1, 128]], compare_op=ALU.is_ge,
                                      fill=MASK_NEG, base=0, channel_multiplier=-1), reads=[c.ZERO], writes=[c.MASK])
    p.op(V, lambda e: e.memset(c.BD, 0.0), writes=[c.BD])
    p.op(V, lambda e: e.memset(CB[0:64, 4, 0:64], 1.0), writes=[CB[0:64, 4, 0:64]])
    p.op(V, lambda e: e.memset(CB[64:128, 4, 64:128], 1.0), writes=[CB[64:128, 4, 64:128]])
    p.op(V, lambda e: e.memset(c.SEL, 0.0), writes=[c.SEL])
    p.op(V, lambda e: e.memset(CB[0:1, 5, :], 1.0), writes=[CB[0:1, 5, :]])
    p.op(V, lambda e: e.memset(CB[32:33, 5, :], 1.0), writes=[CB[32:33, 5, :]])
    p.op(V, lambda e: e.memset(c.EPS_RMS, RMS_EPS), writes=[c.EPS_RMS])
    p.op(V, lambda e: e.memset(c.EPS_LN, LN_EPS), writes=[c.EPS_LN])

    xv = din["xT"].rearrange("(k p) t -> p k t", p=128)
    p.dma("sp", [(XT[:, k, :], xv[:, k, :]) for k in range(8)], "xin")
    p.dma("sp", [(CF[:, 0:32], din["g_mix"].rearrange("p l k -> p (l k)")),
                 (CF[:, 32:64], din["g_mlp"].rearrange("p l k -> p (l k)"))], "misc")

    for kind, l in sublayers:
        if kind == "mlp":
            rmsnorm(c, c.GMLP[:, l, :])
            mlp_phase(c, l)
        elif kind == "gmlp":
            rmsnorm(c, c.GMIX[:, l, :])
            gmlp_phase(c, l)
        elif kind == "fox":
            rmsnorm(c, c.GMIX[:, l, :])
            fox_phase(c, l)

    ov = outT.rearrange("(k p) t -> p k t", p=128)
    p.dma("sp", [(ov[:, k, :], XT[:, k, :]) for k in range(8)], "xout")
    p.wait_all("sp", ["xout"])
    for st in ("pe", "act", "dve", "pool", "sp"):
        p.wait_all(st, [k for k in ("pe", "act", "dve", "pool") if k != st])
    c.es = es
    return nc, c


def next_bank(c, pool=None):
    pool = pool or (0, 1, 2, 3, 4, 5, 6, 7)
    b = pool[c.bank_i % len(pool)]
    c.bank_i += 1
    return c.PS[:, b, :]


def next_slot(c, nslots=4):
    s = c.ring_i % nslots
    c.ring_i += 1
    return s


def load_w(c, src, shape):
    s = next_slot(c, c.ring_slots)
    n = shape[1] * shape[2]
    assert n <= 4096
    dst = c.RING[:, s, 0:n].rearrange("p (a b) -> p a b", a=shape[1])
    c.p.dma("pool", [(dst, src)], f"ring{s}")
    return dst


def rmsnorm(c, G):
    p = c.p
    SQ = c.abf(0, 8 * 512).rearrange("p (k t) -> p k t", k=8)
    LNT = c.af32(8192, 512)
    for tb in range(4):
        ts_ = slice(tb * 512, (tb + 1) * 512)
        xin = c.XT[:, :, ts_]
        p.op("act", lambda e: e.activation(out=SQ, in_=xin, func=AF.Square), reads=[xin], writes=[SQ])
        bank = next_bank(c)
        for k in range(8):
            p.op("pe", lambda e: e.matmul(bank, lhsT=c.ONES, rhs=SQ[:, k, :], start=(k == 0), stop=(k == 7)),
                 reads=[c.ONES, SQ[:, k, :]], writes=[bank], inc=(k == 7))
        p.op("act", lambda e: e.activation(out=LNT, in_=bank, func=AF.Ln, scale=1.0 / D, bias=c.EPS_RMS),
             reads=[bank, c.EPS_RMS], writes=[LNT])
        p.op("act", lambda e: e.activation(out=LNT, in_=LNT, func=AF.Exp, scale=-0.5), reads=[LNT], writes=[LNT])
        for k in range(8):
            o = c.HT[:, k, ts_]
            i0 = c.XT[:, k, ts_]
            p.op("dve", lambda e: e.scalar_tensor_tensor(out=o, in0=i0, scalar=G[:, k:k + 1], in1=LNT,
                                                         op0=ALU.mult, op1=ALU.mult),
                 reads=[i0, G[:, k:k + 1], LNT], writes=[o])


def mlp_phase(c, l):
    p = c.p
    c.ring_slots = 4
    HID = c.abf(0, 32 * 1024).rearrange("p (f t) -> p f t", f=32)
    TMP = [c.af32(65536 + 2048 * i, 512) for i in range(2)]
    w1 = c.din[f"w1_{l}"].rearrange("(k p) f -> p k f", p=128)
    w2r = c.din[f"w2r_{l}"]
    ti = 0
    for tbk in range(2):
        t0 = tbk * 1024
        for fg in range(8):
            W = load_w(c, w1[:, :, fg * 512:(fg + 1) * 512], [128, 8, 512])
            for fi in range(4):
                fc = fg * 4 + fi
                for th in range(2):
                    bank = next_bank(c)
                    rsl = slice(t0 + th * 512, t0 + (th + 1) * 512)
                    for k in range(8):
                        lw = W[:, k, fi * 128:(fi + 1) * 128]
                        rh = c.HT[:, k, rsl]
                        p.op("pe", lambda e: e.matmul(bank, lhsT=lw, rhs=rh, start=(k == 0), stop=(k == 7)),
                             reads=[lw, rh], writes=[bank], inc=(k == 7))
                    tmp = TMP[ti % 2]
                    ti += 1
                    p.op("act", lambda e: e.activation(out=tmp, in_=bank, func=AF.Relu), reads=[bank], writes=[tmp])
                    ho = HID[:, fc, th * 512:(th + 1) * 512]
                    p.op("dve", lambda e: e.tensor_tensor(out=ho, in0=tmp, in1=tmp, op=ALU.mult),
                         reads=[tmp], writes=[ho])
        for dc in range(8):
            W2 = load_w(c, w2r[dc], [128, 32, 128])
            for th in range(2):
                bank = next_bank(c)
                for fc in range(32):
                    lw = W2[:, fc, :]
                    rh = HID[:, fc, th * 512:(th + 1) * 512]
                    p.op("pe", lambda e: e.matmul(bank, lhsT=lw, rhs=rh, start=(fc == 0), stop=(fc == 31)),
                         reads=[lw, rh], writes=[bank], inc=(fc == 31))
                xo = c.XT[:, dc, t0 + th * 512:t0 + (th + 1) * 512]
                p.op("dve", lambda e: e.tensor_tensor(out=xo, in0=bank, in1=xo, op=ALU.add),
                     reads=[bank, xo], writes=[xo])


def gmlp_phase(c, l):
    p = c.p
    c.ring_slots = 4
    OFF = 0
    UT = c.abf(0, 16 * 1024).rearrange("p (c t) -> p c t", c=16)
    VG = [c.af32(32768 + 8192 * i, 2048) for i in range(2)]
    VLN = c.abf(49152, 2048)
    LNG = c.af32(53248, 2048)
    LNB = c.af32(61440, 2048)
    WMT = c.abf(69632, 1024).rearrange("p (g t) -> p g t", g=8)
    BS2 = c.abf(71680, 1024).rearrange("p (g t) -> p g t", g=8)
    WSF = c.af32(73728, 1024).rearrange("p (g t) -> p g t", g=8)
    ST = c.CF[:, 69:69 + 24]
    MV = c.CF[:, 93:95]
    RSTD = c.CF[:, 95:96]
    NMR = c.CF[:, 94:95]
    win = c.din[f"gwin_{l}"].rearrange("(k p) f -> p k f", p=128)
    wor = c.din[f"gwor_{l}"]

    p.dma("sp", [(LNG, c.din[f"glng_{l}"].partition_broadcast(128).rearrange("p o e -> p (o e)")),
                 (LNB, c.din[f"glnb_{l}"].partition_broadcast(128).rearrange("p o e -> p (o e)")),
                 (WSF, c.din[f"gwsT_{l}"])], "misc")
    p.op("pool", lambda e: e.affine_select(out=WMT, in_=WSF, pattern=[[0, 8], [1, 128]], compare_op=ALU.is_ge,
                                           fill=0.0, base=0, channel_multiplier=-1), reads=[WSF], writes=[WMT])
    BSF = c.af32(32768, 1024)
    BSH = c.abf(32768 + 4096, 1024)
    p.op("pool", lambda e: e.memset(BS2[0:64], 0.0), writes=[BS2[0:64]])
    p.dma("sp", [(BSF[0:1, :], c.din[f"gbs_{l}"]), (BSF[32:33, :], c.din[f"gbs_{l}"])], "misc")
    bs2f = BS2.rearrange("p g t -> p (g t)")
    p.op("dve", lambda e: e.tensor_copy(out=bs2f[0:1, :], in_=BSF[0:1, :]), reads=[BSF[0:1, :]], writes=[bs2f[0:1, :]])
    p.op("dve", lambda e: e.tensor_copy(out=BSH[32:33, :], in_=BSF[32:33, :]), reads=[BSF[32:33, :]],
         writes=[BSH[32:33, :]])
    p.op("dve", lambda e: e.tensor_tensor(out=bs2f[32:33, :], in0=BSF[32:33, :], in1=BSH[32:33, :], op=ALU.subtract),
         reads=[BSF[32:33, :], BSH[32:33, :]], writes=[bs2f[32:33, :]])

    import os as _os
    _stop = _os.environ.get("GMLP_STOP", "")
    if _stop == "setup":
        return
    vi = 0
    for tbk in range(2):
        t0 = tbk * 1024
        for ug in range(4):
            W = load_w(c, win[:, :, ug * 512:(ug + 1) * 512], [128, 8, 512])
            for ci in range(4):
                cc = ug * 4 + ci
                for th in range(2):
                    bank = next_bank(c)
                    rsl = slice(t0 + th * 512, t0 + (th + 1) * 512)
                    for k in range(8):
                        lw = W[:, k, ci * 128:(ci + 1) * 128]
                        rh = c.HT[:, k, rsl]
                        p.op("pe", lambda e: e.matmul(bank, lhsT=lw, rhs=rh, start=(k == 0), stop=(k == 7)),
                             reads=[lw, rh], writes=[bank], inc=(k == 7))
                    uo = UT[:, cc, th * 512:(th + 1) * 512]
                    p.op("act", lambda e: e.activation(out=uo, in_=bank, func=AF.Gelu_apprx_tanh),
                         reads=[bank], writes=[uo])
        if _stop == "U" or (_stop == "U2" and tbk == 1):
            return
        WV = [load_w(c, win[:, :, E + cg * 512:E + (cg + 1) * 512], [128, 8, 512]) for cg in range(4)]
        def part_a(n):
            nonlocal vi
            tt = slice(t0 + n * 128, t0 + (n + 1) * 128)
            vg = VG[vi % 2]
            vi += 1
            for cg in range(4):
                bank = next_bank(c)
                for k in range(8):
                    lw = c.HT[:, k, tt]
                    rh = WV[cg][:, k, :]
                    p.op("pe", lambda e: e.matmul(bank, lhsT=lw, rhs=rh, start=(k == 0), stop=(k == 7)),
                         reads=[lw, rh], writes=[bank], inc=(k == 7))
                vo = vg[:, cg * 512:(cg + 1) * 512]
                p.op("act", lambda e: e.activation(out=vo, in_=bank, func=AF.Gelu_apprx_tanh),
                     reads=[bank], writes=[vo])
                so = ST[:, cg * 6:(cg + 1) * 6]
                p.op("dve", lambda e: e.bn_stats(out=so, in_=vo), reads=[vo], writes=[so])
            return vg

        def part_b(vg):
            p.op("dve", lambda e: e.bn_aggr(out=MV, in_=ST), reads=[ST], writes=[MV])
            p.op("act", lambda e: e.activation(out=RSTD, in_=MV[:, 1:2], func=AF.Ln, bias=c.EPS_LN, scale=1.0),
                 reads=[MV[:, 1:2], c.EPS_LN], writes=[RSTD])
            p.op("act", lambda e: e.activation(out=RSTD, in_=RSTD, func=AF.Exp, scale=-0.5), reads=[RSTD], writes=[RSTD])
            p.op("dve", lambda e: e.scalar_tensor_tensor(out=NMR, in0=MV[:, 0:1], scalar=-1.0, in1=RSTD,
                                                         op0=ALU.mult, op1=ALU.mult),
                 reads=[MV[:, 0:1], RSTD], writes=[NMR])
            p.op("act", lambda e: e.activation(out=vg, in_=vg, func=AF.Identity, scale=RSTD, bias=NMR),
                 reads=[vg, RSTD, NMR], writes=[vg])
            p.op("dve", lambda e: e.tensor_tensor(out=vg, in0=vg, in1=LNG, op=ALU.mult), reads=[vg, LNG], writes=[vg])
            p.op("pool", lambda e: e.tensor_tensor(out=VLN, in0=vg, in1=LNB, op=ALU.add), reads=[vg, LNB], writes=[VLN])

        def part_c_mm(n):
            banks = []
            for cq in range(4):
                bank = next_bank(c)
                for ci in range(4):
                    cc = cq * 4 + ci
                    g = cc // 2
                    bo = bank[:, ci * 128:(ci + 1) * 128]
                    lw = VLN[:, cc * 128:(cc + 1) * 128]
                    rh = WMT[:, g, :]
                    p.op("pe", lambda e: e.matmul(bo, lhsT=lw, rhs=rh, start=True, stop=False),
                         reads=[lw, rh], writes=[bo], inc=False)
                    lw2 = c.SEL[0:64, :]
                    rh2 = BS2[0:64, g, :]
                    p.op("pe", lambda e: e.matmul(bo, lhsT=lw2, rhs=rh2, start=False, stop=True),
                         reads=[lw2, rh2], writes=[bo], inc=(ci == 3))
                banks.append(bank)
            return banks

        def part_c_ev(n, banks):
            for cq in range(4):
                bank = banks[cq]
                uo = UT[:, cq * 4:(cq + 1) * 4, n * 128:(n + 1) * 128]
                bi = bank.rearrange("p (a b) -> p a b", a=4)
                p.op("dve", lambda e: e.tensor_tensor(out=uo, in0=bi, in1=uo, op=ALU.mult), reads=[bank, uo], writes=[uo])

        vg_cur = part_a(0)
        part_b(vg_cur)
        for n in range(8):
            vg_next = part_a(n + 1) if n + 1 < 8 else None
            banks = part_c_mm(n)
            if vg_next is not None:
                part_b(vg_next)
            part_c_ev(n, banks)
        if _stop in ("V", "S") or (_stop in ("V2", "S2") and tbk == 1):
            continue
        _o2 = _os.environ.get("GMLP_O2", "all") if tbk == 1 else "all"
        for dcp in range(4):
            WO = load_w(c, wor[dcp], [128, 16, 256])
            if _o2 == "loads":
                continue
            for di in range(2):
                dc = dcp * 2 + di
                for th in range(2):
                    bank = next_bank(c)
                    for kc in range(16):
                        lw = WO[:, kc, di * 128:(di + 1) * 128]
                        rh = UT[:, kc, th * 512:(th + 1) * 512]
                        p.op("pe", lambda e: e.matmul(bank, lhsT=lw, rhs=rh, start=(kc == 0), stop=(kc == 15)),
                             reads=[lw, rh], writes=[bank], inc=(kc == 15))
                    if _o2 == "mm":
                        continue
                    xo = c.XT[:, dc, t0 + th * 512:t0 + (th + 1) * 512]
                    p.op("dve", lambda e: e.tensor_tensor(out=xo, in0=bank, in1=xo, op=ALU.add),
                         reads=[bank, xo], writes=[xo])
        if _stop == "O1":
            return


def fox_phase(c, l):
    p = c.p
    nc = c.nc
    c.ring_slots = 3
    POOL_O = (0, 1)
    POOL_L = (2, 3, 4)
    POOL_P = (5, 6, 7)
    POOL_A = (5, 6, 7, 2, 3, 4, 0, 1)
    VA = c.abf(0, 8 * 16 * 192).rearrange("p (j s c) -> p j s c", j=8, s=16)
    QK = [c.abf(49152 + 4096 * i, 2048) for i in range(4)]
    QE, QO, KE, KO = QK
    SG = c.abf(65536, 2048)
    PT = [c.abf(69632 + 1024 * i, 512) for i in range(3)]
    OG0 = c.abf(72704, 2048)
    SQb = c.RING[:, 3, 0:512]
    LNT = c.ring_f32(3, 1024, 512)
    LND = c.ring_f32(3, 3072, 512)
    O1 = c.ring_f32(3, 5120, 512)
    LF = c.af32(0, 2048)
    CS = c.af32(8192, 2048)
    ONF = c.af32(16384, 2048)
    R1 = c.af32(24576, 2048)
    CH = c.abf(32768, 3 * 2048).rearrange("p (a t) -> p a t", a=3)
    chd = nc.dram_tensor(f"chd_{l}", [16, 3, 2048], BF16, kind="Internal").ap()

    def og(j):
        if j == 0:
            return OG0
        return VA[:, j - 1].rearrange("p s c -> p (s c)")[:, 0:2048]

    p.dma("sp", [(c.GQ, c.din[f"fgq_{l}"]), (c.GK, c.din[f"fgk_{l}"]), (c.NBF[0:16, :], c.din[f"fbf_{l}"])], "misc")
    p.op("dve", lambda e: e.tensor_scalar(out=c.GQ, in0=c.GQ, scalar1=DH ** -0.5, scalar2=None, op0=ALU.mult),
         reads=[c.GQ], writes=[c.GQ])
    p.op("dve", lambda e: e.tensor_scalar(out=c.NBF[0:16, :], in0=c.NBF[0:16, :], scalar1=-1.0, scalar2=None,
                                          op0=ALU.mult), reads=[c.NBF[0:16, :]], writes=[c.NBF[0:16, :]])

    WF = load_w(c, c.din[f"fwf_{l}"].rearrange("p (k h) -> p k h", k=8), [128, 8, 16])
    p.op("pool", lambda e: e.memset(ONF[0:16, :], 1.0), writes=[ONF[0:16, :]])
    for tb in range(4):
        ts_ = slice(tb * 512, (tb + 1) * 512)
        bank = next_bank(c, POOL_P)
        for k in range(8):
            lw = WF[:, k, :]
            rh = c.HT[:, k, ts_]
            p.op("pe", lambda e: e.matmul(bank[0:16, :], lhsT=lw, rhs=rh, start=(k == 0), stop=(k == 7)),
                 reads=[lw, rh], writes=[bank[0:16, :]], inc=(k == 7))
        lo = LF[0:16, ts_]
        p.op("act", lambda e: e.activation(out=lo, in_=bank[0:16, :], func=AF.Exp, scale=-1.0, bias=c.NBF[0:16, :]),
             reads=[bank[0:16, :], c.NBF[0:16, :]], writes=[lo])
        p.op("act", lambda e: e.activation(out=lo, in_=lo, func=AF.Ln, scale=1.0, bias=1.0), reads=[lo], writes=[lo])
    p.op("dve", lambda e: e.tensor_tensor_scan(out=CS[0:16, :], data0=ONF[0:16, :], data1=LF[0:16, :], initial=0.0,
                                               op0=ALU.mult, op1=ALU.add),
         reads=[ONF[0:16, :], LF[0:16, :]], writes=[CS[0:16, :]])
    p.op("dve", lambda e: e.tensor_copy(out=CH[0:16, 0, :], in_=CS[0:16, :]), reads=[CS[0:16, :]], writes=[CH[0:16, 0, :]])
    p.op("dve", lambda e: e.tensor_tensor(out=R1[0:16, :], in0=CS[0:16, :], in1=CH[0:16, 0, :], op=ALU.subtract),
         reads=[CS[0:16, :], CH[0:16, 0, :]], writes=[R1[0:16, :]])
    p.op("dve", lambda e: e.tensor_copy(out=CH[0:16, 1, :], in_=R1[0:16, :]), reads=[R1[0:16, :]], writes=[CH[0:16, 1, :]])
    p.op("dve", lambda e: e.tensor_tensor(out=R1[0:16, :], in0=R1[0:16, :], in1=CH[0:16, 1, :], op=ALU.subtract),
         reads=[R1[0:16, :], CH[0:16, 1, :]], writes=[R1[0:16, :]])
    p.op("dve", lambda e: e.tensor_copy(out=CH[0:16, 2, :], in_=R1[0:16, :]), reads=[R1[0:16, :]], writes=[CH[0:16, 2, :]])
    chd_dep = p.dma("sp", [(chd, CH[0:16, :, :])], "chd")

    for (T, zr, r1, v1) in ((QE, slice(64, 128), slice(96, 99), 1.0), (KE, slice(64, 128), slice(64, 67), -1.0),
                            (QO, slice(0, 64), slice(32, 35), 1.0), (KO, slice(0, 64), slice(0, 3), -1.0)):
        p.op("pool", lambda e: e.memset(T[zr, :], 0.0), writes=[T[zr, :]])
        p.op("pool", lambda e: e.memset(T[r1, :], v1), writes=[T[r1, :]])

    wv = c.din[f"fwv_{l}"].rearrange("(k p) f -> p k f", p=128)
    WV = [load_w(c, wv[:, :, cg * 512:(cg + 1) * 512], [128, 8, 512]) for cg in range(2)]
    VA6 = VA.rearrange("p j s (a d) -> p j s a d", a=3)
    ones_blk = VA6[:, :, :, 1, :]
    for sbk in range(16):
        tt = slice(sbk * 128, (sbk + 1) * 128)
        for cg in range(2):
            bank = next_bank(c, POOL_P)
            for k in range(8):
                lw = c.HT[:, k, tt]
                rh = WV[cg][:, k, :]
                p.op("pe", lambda e: e.matmul(bank, lhsT=lw, rhs=rh, start=(k == 0), stop=(k == 7)),
                     reads=[lw, rh], writes=[bank], inc=(k == 7))
            vo = VA6[:, 4 * cg:4 * cg + 4, sbk, 0:3:2, :]
            bi = bank.rearrange("p (j a d) -> p j a d", j=4, a=2)
            p.op("dve", lambda e: e.tensor_copy(out=vo, in_=bi), reads=[bank], writes=[vo])
    p.op("pool", lambda e: e.memset(ones_blk, 1.0), writes=[ones_blk])

    fq = c.din[f"fqkg_{l}"]
    pti = 0
    for j in range(8):
        W = load_w(c, fq[j].rearrange("p (k c) -> p k c", k=8), [128, 8, 384])
        p.dma("sp", [(QE[64:67, :], chd[2 * j]), (KE[96:99, :], chd[2 * j]),
                     (QO[0:3, :], chd[2 * j + 1]), (KO[32:35, :], chd[2 * j + 1])], "rows", extra_deps=[chd_dep])
        SQs = [SQb, c.RING[:, 3, 2560:3072]]
        LNs = [LNT, LND]
        qi = 0
        for which, (TE, TO, GV) in enumerate(((QE, QO, c.GQ), (KE, KO, c.GK))):
            for tb in range(4):
                ts_ = slice(tb * 512, (tb + 1) * 512)
                SQb_, LNT_ = SQs[qi % 2], LNs[qi % 2]
                qi += 1
                bank = next_bank(c, POOL_A)
                for k in range(8):
                    lw = W[:, k, which * 128:(which + 1) * 128]
                    rh = c.HT[:, k, ts_]
                    p.op("pe", lambda e: e.matmul(bank, lhsT=lw, rhs=rh, start=(k == 0), stop=(k == 7)),
                         reads=[lw, rh], writes=[bank], inc=(k == 7))
                p.op("act", lambda e: e.activation(out=SQb_, in_=bank, func=AF.Square), reads=[bank], writes=[SQb_])
                bank2 = next_bank(c, POOL_A)
                p.op("pe", lambda e: e.matmul(bank2, lhsT=c.BD, rhs=SQb_, start=True, stop=True),
                     reads=[c.BD, SQb_], writes=[bank2])
                p.op("act", lambda e: e.activation(out=LNT_, in_=bank2, func=AF.Ln, scale=1.0 / DH, bias=c.EPS_RMS),
                     reads=[bank2, c.EPS_RMS], writes=[LNT_])
                p.op("act", lambda e: e.activation(out=LNT_, in_=LNT_, func=AF.Exp, scale=-0.5), reads=[LNT_], writes=[LNT_])
                for (T, rs) in ((TE, slice(0, 64)), (TO, slice(64, 128))):
                    o = T[rs, ts_]
                    p.op("dve", lambda e: e.scalar_tensor_tensor(out=o, in0=bank[rs, :], scalar=GV[rs, :], in1=LNT_[rs, :],
                                                                 op0=ALU.mult, op1=ALU.mult),
                         reads=[bank[rs, :], GV[rs, :], LNT_[rs, :]], writes=[o])
        for tb in range(4):
            ts_ = slice(tb * 512, (tb + 1) * 512)
            bank = next_bank(c, POOL_A)
            for k in range(8):
                lw = W[:, k, 256:384]
                rh = c.HT[:, k, ts_]
                p.op("pe", lambda e: e.matmul(bank, lhsT=lw, rhs=rh, start=(k == 0), stop=(k == 7)),
                     reads=[lw, rh], writes=[bank], inc=(k == 7))
            p.op("act", lambda e: e.activation(out=SG[:, ts_], in_=bank, func=AF.Sigmoid), reads=[bank], writes=[SG[:, ts_]])

        OGj = og(j)
        items = []
        for e_ in range(2):
            for g in range(4):
                for sbk in range(4 * g + 4):
                    items.append((e_, g, sbk))
        state = {}

        def emit_qk(it):
            nonlocal pti
            e_, g, sbk = it
            Qt, Kt = (QE, KE) if e_ == 0 else (QO, KO)
            tq0 = g * 512
            c0 = max(0, sbk * 128 - tq0)
            diag = sbk * 128 >= tq0
            lbank = next_bank(c, POOL_L)
            lw = Kt[:, sbk * 128:(sbk + 1) * 128]
            rh = Qt[:, tq0 + c0:tq0 + 512]
            lo = lbank[:, c0:512]
            p.op("pe", lambda e: e.matmul(lo, lhsT=lw, rhs=rh, start=True, stop=not diag),
                 reads=[lw, rh], writes=[lo], inc=not diag)
            if diag:
                lo2 = lbank[:, c0:c0 + 128]
                p.op("pe", lambda e: e.matmul(lo2, lhsT=c.IDENT, rhs=c.MASK, start=False, stop=True),
                     reads=[c.IDENT, c.MASK], writes=[lo2], inc=True)
            pt = PT[pti % 3]
            pti += 1
            po = pt[:, c0:512]
            p.op("act", lambda e: e.activation(out=po, in_=lo, func=AF.Exp), reads=[lo], writes=[po])
            state[it] = (po, c0)

        def emit_pv(it):
            e_, g, sbk = it
            po, c0 = state.pop(it)
            nsb = 4 * g + 4
            if sbk == 0:
                state[("o", e_, g)] = next_bank(c, POOL_O)
            obank = state[("o", e_, g)]
            lw = VA[:, j, sbk, e_ * 64:e_ * 64 + 128]
            oo = obank[:, c0:512]
            p.op("pe", lambda e: e.matmul(oo, lhsT=lw, rhs=po, start=(sbk == 0), stop=(sbk == nsb - 1)),
                 reads=[lw, po], writes=[oo], inc=(sbk == nsb - 1))
            if sbk == nsb - 1:
                pending.append([DEFER, lambda: o_evac(e_, g, obank)])
                del state[("o", e_, g)]

        def o_evac(e_, g, obank):
            if True:
                ro = slice(0, 64) if e_ == 0 else slice(64, 128)
                rd = slice(64, 128) if e_ == 0 else slice(0, 64)
                tq = slice(g * 512, (g + 1) * 512)
                p.op("act", lambda e: e.activation(out=LND[ro, :], in_=obank[rd, :], func=AF.Ln),
                     reads=[obank[rd, :]], writes=[LND[ro, :]])
                p.op("act", lambda e: e.activation(out=LND[ro, :], in_=LND[ro, :], func=AF.Exp, scale=-1.0),
                     reads=[LND[ro, :]], writes=[LND[ro, :]])
                p.op("dve", lambda e: e.tensor_tensor(out=O1[ro, :], in0=obank[ro, :], in1=LND[ro, :], op=ALU.mult),
                     reads=[obank[ro, :], LND[ro, :]], writes=[O1[ro, :]])
                p.op("dve", lambda e: e.tensor_tensor(out=OGj[ro, tq], in0=O1[ro, :], in1=SG[ro, tq], op=ALU.mult),
                     reads=[O1[ro, :], SG[ro, tq]], writes=[OGj[ro, tq]])

        LOOK = 2
        DEFER = 2
        pending = []
        for i in range(min(LOOK, len(items))):
            emit_qk(items[i])
        for i, it in enumerate(items):
            if i + LOOK < len(items):
                emit_qk(items[i + LOOK])
            for pe_ in pending:
                pe_[0] -= 1
            while pending and pending[0][0] <= 0:
                pending.pop(0)[1]()
            emit_pv(it)
        while pending:
            pending.pop(0)[1]()

    wo = c.din[f"fwo_{l}"].rearrange("(k p) f -> p k f", p=128)
    for dh_ in range(2):
        WO = load_w(c, wo[:, :, dh_ * 512:(dh_ + 1) * 512], [128, 8, 512])
        for di in range(4):
            dc = dh_ * 4 + di
            for tb in range(4):
                ts_ = slice(tb * 512, (tb + 1) * 512)
                bank = next_bank(c, POOL_P)
                for kc in range(8):
                    lw = WO[:, kc, di * 128:(di + 1) * 128]
                    rh = og(kc)[:, ts_]
                    p.op("pe", lambda e: e.matmul(bank, lhsT=lw, rhs=rh, start=(kc == 0), stop=(kc == 7)),
                         reads=[lw, rh], writes=[bank], inc=(kc == 7))
                xo = c.XT[:, dc, ts_]
                p.op("dve", lambda e: e.tensor_tensor(out=xo, in0=bank, in1=xo, op=ALU.add), reads=[bank, xo], writes=[xo])


_PROG_CACHE = {}


def _get_prog(sublayers):
    key = tuple(sublayers)
    if key not in _PROG_CACHE:
        _PROG_CACHE[key] = build_program(list(sublayers))
    return _PROG_CACHE[key]


def run_launch(inputs, sublayers, x_cur, cores=NCORES, trace=False):
    nc, c = _get_prog(sublayers)
    in_maps = [prep_inputs(inputs, b, sublayers, x_override=x_cur[b]) for b in range(cores)]
    res = run_bass_kernel_spmd(nc, in_maps, core_ids=list(range(cores)), trace=trace)
    out = np.stack([np.ascontiguousarray(r["outT"].T) for r in res.results], axis=0)
    return out, res


def kernel(**inputs):
    inputs = {k: np.asarray(v) for k, v in inputs.items()}
    x_cur = np.asarray(inputs["x"], dtype=np.float32)
    for sl in LAUNCHES:
        x_cur, _ = run_launch(inputs, sl, x_cur)
    return x_cur.astype(np.float32)
```
